# Optimizing a Trainium2 kernel written in Bass

```python
import math
import jax, jax.numpy as jnp
from jax import lax
import numpy as np

D_MODEL = 1024
BATCH = 2
SEQ = 8192
DEPTH = 1

HEAD_DIM = 64
A_Q_HEADS = 8
A_KV_HEADS = 2
A_GROUP = A_Q_HEADS // A_KV_HEADS
WINDOW = 128
BLOCK = 128
B_HEADS = 4
B_VDIM = 2 * HEAD_DIM
A_WIDTH = A_Q_HEADS * HEAD_DIM
B_WIDTH = B_HEADS * B_VDIM
D_FF = 256 * ((8 * D_MODEL // 3 + 255) // 256)
ROPE_THETA = 10000.0
EPS = 1e-6
N_MOD = 9
NEG = -1e30

SPLIT_SIZES = (
    A_Q_HEADS * HEAD_DIM,
    A_KV_HEADS * HEAD_DIM,
    A_KV_HEADS * HEAD_DIM,
    B_HEADS * 2 * HEAD_DIM,
    B_HEADS * 2 * HEAD_DIM,
    B_HEADS * B_VDIM,
    D_MODEL,
    D_MODEL,
)
IN_COLS = sum(SPLIT_SIZES)
SPLIT_POINTS = tuple(int(v) for v in np.cumsum(SPLIT_SIZES)[:-1])

kernel_name = "hybrid_gated_window_diff_attn_macaron_adaln"


def rmsnorm(x, g):
    xf = x.astype(jnp.float32)
    y = xf * lax.rsqrt(jnp.mean(xf * xf, axis=-1, keepdims=True) + EPS)
    return (y * g.astype(jnp.float32)).astype(x.dtype)


def modulate(h, shift, scale):
    return h * (1 + scale[:, None, :]) + shift[:, None, :]


def swiglu(h, w_in, w_out):
    gate, up = jnp.split(h @ w_in, 2, axis=-1)
    return (jax.nn.silu(gate) * up) @ w_out


def rope_tables(positions):
    inv_freq = ROPE_THETA ** (-jnp.arange(0, HEAD_DIM, 2, dtype=jnp.float32) / HEAD_DIM)
    ang = positions.astype(jnp.float32)[..., None] * inv_freq
    ang = jnp.concatenate([ang, ang], axis=-1)
    return jnp.cos(ang), jnp.sin(ang)


def apply_rope(x, cos, sin):
    bshape = cos.shape[:2] + (1,) * (x.ndim - 3) + (HEAD_DIM,)
    cos, sin = cos.reshape(bshape), sin.reshape(bshape)
    xf = x.astype(jnp.float32)
    x1, x2 = jnp.split(xf, 2, axis=-1)
    rot = jnp.concatenate([-x2, x1], axis=-1)
    return (xf * cos + rot * sin).astype(x.dtype)


def window_gqa_sink(q, k, v, sink):
    B, S = q.shape[0], q.shape[1]
    nb = S // BLOCK
    qb = q.reshape(B, nb, BLOCK, A_KV_HEADS, A_GROUP, HEAD_DIM)
    pad = ((0, 0), (BLOCK, BLOCK), (0, 0), (0, 0))
    kp = jnp.pad(k, pad).reshape(B, nb + 2, BLOCK, A_KV_HEADS, HEAD_DIM)
    vp = jnp.pad(v, pad).reshape(B, nb + 2, BLOCK, A_KV_HEADS, HEAD_DIM)
    kw = jnp.concatenate([kp[:, :-2], kp[:, 1:-1], kp[:, 2:]], axis=2)
    vw = jnp.concatenate([vp[:, :-2], vp[:, 1:-1], vp[:, 2:]], axis=2)
    s = jnp.einsum('bnqhgd,bnkhd->bnhgqk', qb, kw,
                   preferred_element_type=jnp.float32) * (HEAD_DIM ** -0.5)
    blk = jnp.arange(nb)[:, None, None] * BLOCK
    qpos = blk + jnp.arange(BLOCK)[None, :, None]
    kpos = blk - BLOCK + jnp.arange(3 * BLOCK)[None, None, :]
    valid = (jnp.abs(qpos - kpos) <= WINDOW) & (kpos >= 0) & (kpos < S)
    s = jnp.where(valid[None, :, None, None], s, NEG)
    sk = sink.astype(jnp.float32).reshape(A_KV_HEADS, A_GROUP)[None, None, :, :, None, None]
    m = jnp.maximum(jnp.max(s, axis=-1, keepdims=True), sk)
    e = jnp.exp(s - m)
    p = e / (jnp.sum(e, axis=-1, keepdims=True) + jnp.exp(sk - m))
    o = jnp.einsum('bnhgqk,bnkhd->bnqhgd', p.astype(v.dtype), vw)
    return o.reshape(B, S, A_Q_HEADS * HEAD_DIM)


def diff_attention(q, k, v, lam):
    B, S = q.shape[0], q.shape[1]
    nb = S // BLOCK
    qb = q.reshape(B, nb, BLOCK, B_HEADS, 2, HEAD_DIM).transpose(1, 0, 2, 3, 4, 5)

    def one_block(qblk):
        s = jnp.einsum('bqhtd,bkhtd->bhtqk', qblk, k,
                       preferred_element_type=jnp.float32) * (HEAD_DIM ** -0.5)
        p = jax.nn.softmax(s, axis=-1)
        a = p[:, :, 0] - lam * p[:, :, 1]
        return jnp.einsum('bhqk,bkhe->bqhe', a.astype(v.dtype), v)

    o = lax.map(one_block, qb)
    return o.transpose(1, 0, 2, 3, 4).reshape(B, S, B_HEADS, B_VDIM)


def setup_inputs(seed: int = 0) -> dict:
    key = jax.random.key(seed)
    ks = iter(jax.random.split(key, 40))
    f32 = jnp.float32

    def nrm(shape, scale):
        return jax.random.normal(next(ks), shape, f32) * scale

    def gain(shape):
        return 1.0 + 0.02 * jax.random.normal(next(ks), shape, f32)

    L = DEPTH
    return {
        "x": nrm((BATCH, SEQ, D_MODEL), 1.0),
        "c": nrm((BATCH, D_MODEL), 1.0),
        "positions": jnp.broadcast_to(jnp.arange(SEQ, dtype=jnp.int32), (BATCH, SEQ)),
        "w_mod": nrm((L, D_MODEL, N_MOD * D_MODEL), 0.5 * D_MODEL ** -0.5),
        "b_mod": nrm((L, N_MOD * D_MODEL), 0.02),
        "norm_ffn1": gain((L, D_MODEL)),
        "w_ffn1_in": nrm((L, D_MODEL, 2 * D_FF), D_MODEL ** -0.5),
        "w_ffn1_out": nrm((L, D_FF, D_MODEL), D_FF ** -0.5),
        "norm_mix": gain((L, D_MODEL)),
        "w_in": nrm((L, D_MODEL, IN_COLS), D_MODEL ** -0.5),
        "qn_a": gain((L, HEAD_DIM)),
        "kn_a": gain((L, HEAD_DIM)),
        "sink_a": nrm((L, A_Q_HEADS), 1.0),
        "qn_b": gain((L, HEAD_DIM)),
        "kn_b": gain((L, HEAD_DIM)),
        "lam_q1": nrm((L, HEAD_DIM), 0.1),
        "lam_k1": nrm((L, HEAD_DIM), 0.1),
        "lam_q2": nrm((L, HEAD_DIM), 0.1),
        "lam_k2": nrm((L, HEAD_DIM), 0.1),
        "subln_b": gain((L, B_VDIM)),
        "w_branch_a": nrm((L, A_WIDTH, D_MODEL), A_WIDTH ** -0.5),
        "w_branch_b": nrm((L, B_WIDTH, D_MODEL), B_WIDTH ** -0.5),
        "w_out": nrm((L, D_MODEL, D_MODEL), D_MODEL ** -0.5),
        "norm_ffn2": gain((L, D_MODEL)),
        "w_ffn2_in": nrm((L, D_MODEL, 2 * D_FF), D_MODEL ** -0.5),
        "w_ffn2_out": nrm((L, D_FF, D_MODEL), D_FF ** -0.5),
    }


def reference(x, c, positions, w_mod, b_mod, norm_ffn1, w_ffn1_in, w_ffn1_out,
              norm_mix, w_in, qn_a, kn_a, sink_a, qn_b, kn_b,
              lam_q1, lam_k1, lam_q2, lam_k2, subln_b,
              w_branch_a, w_branch_b, w_out, norm_ffn2, w_ffn2_in, w_ffn2_out):
    B, S = x.shape[0], x.shape[1]
    cos, sin = rope_tables(positions)
    c_act = jax.nn.silu(c)
    for l in range(DEPTH):
        mod = c_act @ w_mod[l] + b_mod[l]
        (sh1, sc1, g1, sh2, sc2, g2, sh3, sc3, g3) = jnp.split(mod, N_MOD, axis=-1)

        h = modulate(rmsnorm(x, norm_ffn1[l]), sh1, sc1)
        x = x + 0.5 * g1[:, None, :] * swiglu(h, w_ffn1_in[l], w_ffn1_out[l])

        h = modulate(rmsnorm(x, norm_mix[l]), sh2, sc2)
        qa, ka, va, qb, kb, vb, ga, gb = jnp.split(h @ w_in[l], SPLIT_POINTS, axis=-1)

        qa = apply_rope(rmsnorm(qa.reshape(B, S, A_Q_HEADS, HEAD_DIM), qn_a[l]), cos, sin)
        ka = apply_rope(rmsnorm(ka.reshape(B, S, A_KV_HEADS, HEAD_DIM), kn_a[l]), cos, sin)
        va = va.reshape(B, S, A_KV_HEADS, HEAD_DIM)
        ya = window_gqa_sink(qa, ka, va, sink_a[l]) @ w_branch_a[l]

        qb = apply_rope(rmsnorm(qb.reshape(B, S, B_HEADS, 2, HEAD_DIM), qn_b[l]), cos, sin)
        kb = apply_rope(rmsnorm(kb.reshape(B, S, B_HEADS, 2, HEAD_DIM), kn_b[l]), cos, sin)
        vb = vb.reshape(B, S, B_HEADS, B_VDIM)
        lam_init = 0.8 - 0.6 * math.exp(-0.3 * l)
        f32 = jnp.float32
        lam = (jnp.exp(jnp.sum(lam_q1[l].astype(f32) * lam_k1[l].astype(f32)))
               - jnp.exp(jnp.sum(lam_q2[l].astype(f32) * lam_k2[l].astype(f32))) + lam_init)
        ob = diff_attention(qb, kb, vb, lam)
        ob = (rmsnorm(ob, subln_b[l]) * (1.0 - lam_init)).reshape(B, S, B_WIDTH)
        yb = ob @ w_branch_b[l]

        merged = jax.nn.sigmoid(ga) * ya + jax.nn.sigmoid(gb) * yb
        x = x + g2[:, None, :] * (merged @ w_out[l])

        h = modulate(rmsnorm(x, norm_ffn2[l]), sh3, sc3)
        x = x + 0.5 * g3[:, None, :] * swiglu(h, w_ffn2_in[l], w_ffn2_out[l])
    return x
```

```python
import math
import numpy as np
import ml_dtypes
from contextlib import ExitStack
import concourse.bass as bass
import concourse.mybir as mybir
from concourse.bass_utils import run_bass_kernel_spmd

F32 = mybir.dt.float32
BF16 = mybir.dt.bfloat16
I32 = mybir.dt.int32
AF = mybir.ActivationFunctionType
ALU = mybir.AluOpType

ENGS = ("pe", "act", "dve", "pool", "sp")

D = 1024
NT = 2048
NG = 4
GT = 512
DFF = 2816
NHC = 22
EPS = 1e-6
LAM_INIT = 0.8 - 0.6 * math.exp(-0.3 * 0)
MAGIC = 12582912.0
TWO_PI = 2.0 * math.pi
C1 = 6.28125
C2 = TWO_PI - C1
PI_CL = 3.1415925

OQA, OKA, OVA, OQB, OKB, OVB, OGA, OGB = 0, 512, 640, 768, 1280, 1792, 2304, 3328

CC = {}
_o = 0
for _n, _w in [("cT", 8), ("bmod", 72), ("gains", 24), ("qkg", 4), ("qkgb", 256), ("sink", 8), ("lam", 256),
               ("subln", 128), ("invf", 1), ("sel", 16), ("ident", 128), ("ones", 128), ("bones", 128), ("rotm", 128)]:
    CC[_n] = (_o, _o + _w)
    _o += _w
NCC = _o
MB = {"identb": (0, 128), "mprev": (128, 640), "mnext": (640, 1152)}
NMB = 1152


class Buf:
    __slots__ = ("name",)

    def __init__(self, name):
        self.name = name


class Op:
    __slots__ = ("eng", "fn", "deps", "sig", "sem", "val", "kind", "name")


class Sched:
    NDMA = 16

    def __init__(self, nc):
        self.nc = nc
        self.ops = {e: [] for e in ENGS}
        self.last_w = {}
        self.readers = {}

    def add(self, eng, fn, reads=(), writes=(), kind="c", name=""):
        op = Op()
        op.eng, op.fn, op.kind, op.name = eng, fn, kind, name
        op.sig = False
        op.sem = None
        op.val = 0
        deps = []
        seen = set()

        def adddep(d):
            if d is None or d is op or id(d) in seen:
                return
            if eng == "pe" and d.eng == "pe" and d.kind == "c" and kind == "c":
                return
            seen.add(id(d))
            deps.append(d)

        for b in reads:
            adddep(self.last_w.get(b))
        for b in writes:
            adddep(self.last_w.get(b))
            for r in self.readers.get(b, ()):
                adddep(r)
        op.deps = deps
        for d in deps:
            d.sig = True
        for b in reads:
            self.readers.setdefault(b, []).append(op)
        for b in writes:
            self.last_w[b] = op
            self.readers[b] = []
        self.ops[eng].append(op)
        return op

    def inherit(self, new, olds):
        lst = self.readers.setdefault(new, [])
        for o in olds:
            w = self.last_w.get(o)
            if w is not None:
                lst.append(w)
            lst.extend(self.readers.get(o, ()))

    def emit(self, stack, final_ops=()):
        nc = self.nc
        EPOCH = 1500
        nsig = {e: sum(1 for o in self.ops[e] if o.sig and o.kind == "c") for e in ENGS}
        sems = {e: [stack.enter_context(nc.semaphore(f"s_{e}{i}")) for i in range(nsig[e] // EPOCH + 1)] for e in ENGS}
        ccsem = stack.enter_context(nc.semaphore("s_cc"))
        dsems = {e: [stack.enter_context(nc.semaphore(f"d_{e}{i}")) for i in range(self.NDMA)]
                 for e in ("sp", "pool", "act")}
        cnt = {e: 0 for e in ENGS}
        dcnt = {e: [0] * self.NDMA for e in dsems}
        drot = {e: 0 for e in dsems}
        cccnt = 0
        for e in ENGS:
            for op in self.ops[e]:
                if op.kind == "dma":
                    i = drot[e] % self.NDMA
                    drot[e] += 1
                    dcnt[e][i] += 16
                    op.sem, op.val, op.sig = dsems[e][i], dcnt[e][i], True
                    op.name = dcnt[e][i] - 16
                elif op.kind == "cc":
                    cccnt += 1
                    op.sem, op.val, op.sig = ccsem, cccnt, True
                elif op.sig:
                    op.sem, op.val = sems[e][cnt[e] // EPOCH], cnt[e] % EPOCH + 1
                    cnt[e] += 1
        block = stack.enter_context(nc.Block())
        handles = {"pe": block.tensor, "act": block.scalar, "dve": block.vector,
                   "pool": block.gpsimd, "sp": block.sync}

        def make(e):
            def body(eng):
                waited = {}
                for op in self.ops[e]:
                    for d in op.deps:
                        k = id(d.sem)
                        if waited.get(k, 0) >= d.val:
                            continue
                        eng.wait_ge(d.sem, d.val)
                        waited[k] = d.val
                    if op.kind == "dma" and op.name and waited.get(id(op.sem), 0) < op.name:
                        eng.wait_ge(op.sem, op.name)
                        waited[id(op.sem)] = op.name
                    inst = op.fn(eng)
                    if op.sig:
                        inst.then_inc(op.sem, 16 if op.kind == "dma" else 1)
                if e == "sp":
                    for q_ in dsems:
                        for i_, sm_ in enumerate(dsems[q_]):
                            if dcnt[q_][i_] > waited.get(id(sm_), 0):
                                eng.wait_ge(sm_, dcnt[q_][i_])
                    if cccnt:
                        eng.wait_ge(ccsem, cccnt)
            return body

        for e in ENGS:
            handles[e](make(e))


class Arena:
    def __init__(self, t, nbytes, sched):
        self.t, self.n, self.top, self.S = t, nbytes, 0, sched
        self.live, self.dead = [], []

    def alloc(self, shape, dt, bufs):
        esz = 2 if dt == BF16 else 4
        ne = int(np.prod(shape))
        nb = (ne * esz + 63) // 64 * 64
        off = self.top
        self.top += nb
        assert self.top <= self.n, ("arena overflow", self.top, self.n)
        for (o_, e_, ob) in self.dead:
            if o_ < off + nb and off < e_:
                for b in bufs:
                    self.S.inherit(b, ob)
        self.live.append((off, off + nb, list(bufs)))
        v = self.t[:, off // 4: (off + nb) // 4]
        if dt != F32:
            v = v.bitcast(dt)
        v = v[:, 0:ne]
        if len(shape) == 2:
            v = v.rearrange("p (a b) -> p a b", b=shape[1])
        elif len(shape) == 3:
            v = v.rearrange("p (a b c) -> p a b c", b=shape[1], c=shape[2])
        return v

    def mark(self):
        return self.top

    def release(self, m):
        keep = []
        for a in self.live:
            (keep if a[0] < m else self.dead).append(a)
        self.live = keep
        self.top = m


DEBUG = False
SUB = 0
W2A = 0
W2P = 99
M1B = 0
NOCC = False
DBGW = 8192


def build(stage=3):
    nc = bass.Bass("TRN2", target_bir_lowering=False)
    dbg_d = nc.dram_tensor("dbg", [128, DBGW], F32, kind="ExternalOutput").ap() if DEBUG else None
    dbg_state = {"off": 0, "ops": [], "names": []}
    dt_in = lambda n, s, d=F32: nc.dram_tensor(n, s, d, kind="ExternalInput").ap()
    x_d = dt_in("x", [NT, D])
    pos_d = dt_in("pos", [128, NT], I32)
    cst_d = dt_in("cst", [128, NCC])
    msk_d = dt_in("msk", [128, NMB], BF16)
    wmod_d = dt_in("w_mod", [D, 9 * D])
    w1i_d = dt_in("w_ffn1_in", [D, 2 * DFF])
    w1o_d = dt_in("w_ffn1_out", [DFF, D])
    win_d = dt_in("w_in", [D, 4352])
    wa_d = dt_in("w_branch_a", [512, D])
    wb_d = dt_in("w_branch_b", [512, D])
    wo_d = dt_in("w_out", [D, D])
    w2i_d = dt_in("w_ffn2_in", [D, 2 * DFF])
    w2o_d = dt_in("w_ffn2_out", [DFF, D])
    out_d = nc.dram_tensor("out", [NT, D], F32, kind="ExternalOutput").ap()
    kgi = [nc.dram_tensor(f"kg_in{i}", [128, 2048], BF16, kind="Internal").ap() for i in range(6)]
    kgo = [nc.dram_tensor(f"kg_out{i}", [512, 2048], BF16, kind="Internal").ap() for i in range(6)]
    vgi = [nc.dram_tensor(f"vg_in{i}", [128, 2112], BF16, kind="Internal").ap() for i in range(5)]
    vgo = [nc.dram_tensor(f"vg_out{i}", [512, 2112], BF16, kind="Internal").ap() for i in range(5)]

    S = Sched(nc)
    bufs = {}

    def B(*k):
        if k not in bufs:
            bufs[k] = Buf(str(k))
        return bufs[k]

    with ExitStack() as st:
        ARENA_BYTES = 206 * 1024
        arena_t = st.enter_context(nc.sbuf_tensor("arena", [128, ARENA_BYTES // 4], F32))
        A = Arena(arena_t, ARENA_BYTES, S)
        PSD = [st.enter_context(nc.psum_tensor(f"psd{i}", [128, 1024], F32)) for i in range(4)]
        PS = [PSD[i // 2][:, (i % 2) * 512:(i % 2 + 1) * 512] for i in range(8)]
        PB = [B("ps", i) for i in range(8)]

        xT = A.alloc([8, NT], F32, [B("xT", c, G) for c in range(8) for G in range(NG)])
        hT = A.alloc([8, NT], BF16, [B("hT", c, G) for c in range(8) for G in range(NG)])
        cst = A.alloc([NCC], F32, [B("cst")])
        msk = A.alloc([NMB], BF16, [B("msk")])
        modT = A.alloc([72], F32, [B("modT", (0,)), B("modT", (1,)), B("modT", (2,))])
        scal = A.alloc([72], F32, [B("scal", 0), B("scal", 1), B("scal", 2)])
        misc = A.alloc([64], F32, [B("misc")])

        def cs(name, a=None, b=None):
            lo, hi = CC[name]
            if a is None:
                return cst[:, lo:hi]
            return cst[:, lo + a: lo + b]

        def mk(name, a=None, b=None):
            lo, hi = MB[name]
            if a is None:
                return msk[:, lo:hi]
            return msk[:, lo + a: lo + b]

        ident = cs("ident")
        identb = mk("identb")

        def dma(eng, out, in_, reads, writes):
            return S.add(eng, lambda e: e.dma_start(out=out, in_=in_), reads, writes, kind="dma")

        def mm(out, lhsT, rhs, start, stop, reads, writes, skip=False):
            if skip:
                return S.add("pe", lambda e: e.matmul(out, lhsT=lhsT, rhs=rhs, start=start, stop=stop,
                                                      skip_group_check=True), reads, writes)
            return S.add("pe", lambda e: e.matmul(out, lhsT=lhsT, rhs=rhs, start=start, stop=stop), reads, writes)

        def tr(out, in_, idn, reads, writes):
            return S.add("pe", lambda e: e.transpose(out, in_, idn), reads, writes)

        def act(out, in_, func, reads, writes, bias=None, scale=None):
            kw = {}
            if bias is not None:
                kw["bias"] = bias
            if scale is not None:
                kw["scale"] = scale
            return S.add("act", lambda e: e.activation(out=out, in_=in_, func=func, **kw), reads, writes)

        def tt(eng, out, in0, in1, op, reads, writes):
            return S.add(eng, lambda e: e.tensor_tensor(out=out, in0=in0, in1=in1, op=op), reads, writes)

        def ts(eng, out, in0, s1, s2, op0, op1, reads, writes):
            if op1 is None:
                return S.add(eng, lambda e: e.tensor_scalar(out=out, in0=in0, scalar1=s1, scalar2=None, op0=op0),
                             reads, writes)
            return S.add(eng, lambda e: e.tensor_scalar(out=out, in0=in0, scalar1=s1, scalar2=s2, op0=op0, op1=op1),
                         reads, writes)

        def stt(out, in0, scalar, in1, op0, op1, reads, writes):
            return S.add("dve", lambda e: e.scalar_tensor_tensor(out=out, in0=in0, scalar=scalar, in1=in1,
                                                                  op0=op0, op1=op1), reads, writes)

        def cp(eng, out, in_, reads, writes):
            if eng == "act":
                return S.add("act", lambda e: e.copy(out=out, in_=in_), reads, writes)
            return S.add(eng, lambda e: e.tensor_copy(out=out, in_=in_), reads, writes)

        dbg_st = A.alloc([512], F32, [B("dbgst")]) if DEBUG else None

        def dump(name, ap, w, reads):
            if not DEBUG:
                return
            cp("dve", dbg_st[:, 0:w], ap, reads, [B("dbgst")])
            o_ = dbg_state["off"]
            dbg_state["ops"].append(dma("sp", dbg_d[:, o_:o_ + w], dbg_st[:, 0:w], [B("dbgst")], [B("dbgout")]))
            dbg_state["names"].append((name, o_, w))
            dbg_state["off"] = o_ + w

        dma("sp", cst, cst_d, [], [B("cst")])
        dma("sp", msk, msk_d, [], [B("msk")])
        m0 = A.mark()
        cact = A.alloc([8], BF16, [B("cact")])
        act(cact, cs("cT"), AF.Silu, [B("cst")], [B("cact")])
        wm = [A.alloc([8, 1024], BF16, [B("wm", i_)]) for i_ in range(2)]

        def mod_part(vs, sub):
            for v in vs:
                wbuf = wm[v % 2]
                dma("pool", wbuf, wmod_d[:, v * 1024:(v + 1) * 1024].rearrange("(k p) n -> p k n", p=128),
                    [], [B("wm", v % 2)])
                for j in range(8):
                    for k in range(8):
                        mm(PS[7][:, v * 8 + j: v * 8 + j + 1], wbuf[:, k, j * 128:(j + 1) * 128], cact[:, k:k + 1],
                           k == 0, k == 7, [B("wm", v % 2), B("cact")], [PB[7]])
            lo, hi = vs[0] * 8, vs[-1] * 8 + 8
            tt("dve", modT[:, lo:hi], PS[7][:, lo:hi], cs("bmod", lo, hi), ALU.add, [PB[7], B("cst")], [B("modT", sub)])
            for i in sub:
                o = i * 24
                stt(scal[:, o:o + 8], modT[:, o + 8:o + 16], 1.0, cs("gains", i * 8, i * 8 + 8), ALU.add, ALU.mult,
                    [B("modT", sub), B("cst")], [B("scal", i)])
                cp("dve", scal[:, o + 8:o + 16], modT[:, o:o + 8], [B("modT", sub)], [B("scal", i)])
                ts("dve", scal[:, o + 16:o + 24], modT[:, o + 16:o + 24], 1.0 if i == 1 else 0.5, None, ALU.mult, None,
                   [B("modT", sub)], [B("scal", i)])

        mod_part([0, 1, 2], (0,))
        mM0 = m0

        m0 = A.mark()
        xst = [A.alloc([D], F32, [B("xst", i_)]) for i_ in range(3)]
        for t_ in range(16):
            xs = xst[t_ % 3]
            G = t_ // 4
            dma("sp", xs, x_d[t_ * 128:(t_ + 1) * 128, :], [], [B("xst", t_ % 3)])
            for c in range(8):
                tr(PS[c][:, (t_ % 4) * 128:(t_ % 4 + 1) * 128], xs[:, c * 128:(c + 1) * 128], ident,
                   [B("xst", t_ % 3), B("cst")], [PB[c]])
            if t_ % 4 == 3:
                for c in range(8):
                    cp("act" if c % 2 else "dve", xT[:, c, G * GT:(G + 1) * GT], PS[c][:, :], [PB[c]], [B("xT", c, G)])
        dump("xT00", xT[:, 0, 0:512], 512, [B("xT", 0, 0)])
        dump("xT73", xT[:, 7, 1536:2048], 512, [B("xT", 7, 3)])
        A.release(m0)

        def norm_mod(i):
            o = i * 24
            m0 = A.mark()
            sq = [A.alloc([GT], F32, [B("nsq", i_)]) for i_ in range(2)]
            lnv = A.alloc([GT], F32, [B("nln")])
            rstd = A.alloc([GT], F32, [B("nrstd")])
            tmp = [A.alloc([GT], F32, [B("ntmp", i_)]) for i_ in range(2)]
            for G in range(NG):
                gs = slice(G * GT, (G + 1) * GT)
                for c in range(8):
                    act(sq[c % 2], xT[:, c, gs], AF.Square, [B("xT", c, G)], [B("nsq", c % 2)])
                    mm(PS[0][:, :], cs("ones"), sq[c % 2], c == 0, c == 7, [B("nsq", c % 2), B("cst")], [PB[0]])
                act(lnv, PS[0][:, :], AF.Ln, [PB[0], B("misc")], [B("nln")], bias=misc[:, 0:1], scale=1.0 / D)
                act(rstd, lnv, AF.Exp, [B("nln")], [B("nrstd")], scale=-0.5)
                for c in range(8):
                    stt(tmp[c % 2], xT[:, c, gs], scal[:, o + c:o + c + 1], rstd, ALU.mult, ALU.mult,
                        [B("xT", c, G), B("scal", i), B("nrstd")], [B("ntmp", c % 2)])
                    act(hT[:, c, gs], tmp[c % 2], AF.Identity, [B("ntmp", c % 2), B("scal", i)], [B("hT", c, G)],
                        bias=scal[:, o + 8 + c:o + 9 + c], scale=1.0)
                if G == 0 and i == 0:
                    dump("rstd0", rstd, 512, [B("nrstd")])
                    dump("hT00", hT[:, 0, 0:512], 512, [B("hT", 0, 0)])
                    dump("hT70", hT[:, 7, 0:512], 512, [B("hT", 7, 0)])
            A.release(m0)

        S.add("dve", lambda e: e.memset(misc[:, 0:1], EPS), [], [B("misc")])

        def ffn(i, wi_d, wo_d_):
            o = i * 24
            m0 = A.mark()
            NS = 3
            slab = [(A.alloc([8, 256], BF16, [B("fslab", i_, 0)]), A.alloc([8, 256], BF16, [B("fslab", i_, 1)]),
                     A.alloc([2, 1024], BF16, [B("fslab", i_, 2)])) for i_ in range(NS)]
            sg = [A.alloc([GT], F32, [B("fsg", i_)]) for i_ in range(2)]
            aT = [A.alloc([GT], BF16, [B("faT", i_)]) for i_ in range(4)]
            for s in range(NHC // 2):
                wg, wu, wo_ = slab[s % NS]
                sbg, sbu, sbo = B("fslab", s % NS, 0), B("fslab", s % NS, 1), B("fslab", s % NS, 2)
                j0 = 2 * s
                dma("pool", wg, wi_d[:, j0 * 128:j0 * 128 + 256].rearrange("(k p) n -> p k n", p=128), [], [sbg])
                dma("pool", wu, wi_d[:, DFF + j0 * 128:DFF + j0 * 128 + 256].rearrange("(k p) n -> p k n", p=128),
                    [], [sbu])
                dma("pool", wo_, wo_d_[j0 * 128:j0 * 128 + 256, :].rearrange("(j p) n -> p j n", p=128), [], [sbo])
                for G in range(NG):
                    gs = slice(G * GT, (G + 1) * GT)
                    q = s * NG + G
                    for jj in range(2):
                        it = q * 2 + jj
                        pg, pu = 1 + (it % 2), 3 + (it % 2)
                        a_ = aT[(q % 2) * 2 + jj]
                        ab = B("faT", (q % 2) * 2 + jj)
                        for k in range(8):
                            mm(PS[pg][:, :], wg[:, k, jj * 128:(jj + 1) * 128], hT[:, k, gs], k == 0, k == 7,
                               [sbg, B("hT", k, G)], [PB[pg]])
                        for k in range(8):
                            mm(PS[pu][:, :], wu[:, k, jj * 128:(jj + 1) * 128], hT[:, k, gs], k == 0, k == 7,
                               [sbu, B("hT", k, G)], [PB[pu]])
                        act(sg[it % 2], PS[pg][:, :], AF.Silu, [PB[pg]], [B("fsg", it % 2)])
                        tt("dve", a_, sg[it % 2], PS[pu][:, :], ALU.mult, [B("fsg", it % 2), PB[pu]], [ab])
                    for c2 in range(8):
                        py = 5 + ((q * 8 + c2) % 3)
                        for jj in range(2):
                            a_ = aT[(q % 2) * 2 + jj]
                            ab = B("faT", (q % 2) * 2 + jj)
                            mm(PS[py][:, :], wo_[:, jj, c2 * 128:(c2 + 1) * 128], a_, jj == 0, jj == 1,
                               [sbo, ab], [PB[py]])
                        stt(xT[:, c2, gs], PS[py][:, :], scal[:, o + 16 + c2:o + 17 + c2], xT[:, c2, gs],
                            ALU.mult, ALU.add, [PB[py], B("scal", i), B("xT", c2, G)], [B("xT", c2, G)])
            A.release(m0)

        norm_mod(0)
        ffn(0, w1i_d, w1o_d)
        mod_part([3, 4, 5], (1,))
        if stage >= 2:
            norm_mod(1)
        mod_part([6, 7, 8], (2,))
        A.release(mM0)


        AX = mybir.AxisListType.X

        def mixer():
            mM = A.mark()
            QT = A.alloc([8, NT], BF16, [B("QT", ch, j) for ch in range(8) for j in range(16)])
            Vast = A.alloc([2112], BF16, [B("Vast")])
            Vast4 = Vast.rearrange("p (t g e) -> p t g e", g=2, e=66)
            sm = A.alloc([64], F32, [B("sm")])
            subg = A.alloc([128], F32, [B("subg")])
            ltmp = A.alloc([128], F32, [B("ltmp")])
            QB = lambda ch, G: [B("QT", ch, j) for j in range(4 * G, 4 * G + 4)]
            S.add("dve", lambda e: e.tensor_reduce(out=sm[:, 0:4], in_=cs("qkgb").rearrange("p (a b) -> p a b", b=64),
                                                    axis=AX, op=ALU.max, apply_absolute_value=True),
                  [B("cst")], [B("sm")])
            tt("dve", sm[:, 4:5], sm[:, 0:1], sm[:, 1:2], ALU.mult, [B("sm")], [B("sm")])
            tt("dve", sm[:, 5:6], sm[:, 2:3], sm[:, 3:4], ALU.mult, [B("sm")], [B("sm")])
            ts("dve", sm[:, 4:6], sm[:, 4:6], -8.0, None, ALU.mult, None, [B("sm")], [B("sm")])
            act(sm[:, 8:16], cs("sink"), AF.Exp, [B("cst"), B("sm")], [B("sm")], bias=sm[:, 4:5], scale=1.0)
            tt("dve", ltmp[:, 0:64], cs("lam", 0, 64), cs("lam", 64, 128), ALU.mult, [B("cst")], [B("ltmp")])
            tt("dve", ltmp[:, 64:128], cs("lam", 128, 192), cs("lam", 192, 256), ALU.mult, [B("cst")], [B("ltmp")])
            S.add("dve", lambda e: e.tensor_reduce(out=sm[:, 16:18], in_=ltmp.rearrange("p (a b) -> p a b", b=64),
                                                    axis=AX, op=ALU.add), [B("ltmp")], [B("sm")])
            act(sm[:, 18:20], sm[:, 16:18], AF.Exp, [B("sm")], [B("sm")])
            tt("dve", sm[:, 20:21], sm[:, 18:19], sm[:, 19:20], ALU.subtract, [B("sm")], [B("sm")])
            ts("dve", sm[:, 21:22], sm[:, 20:21], LAM_INIT, -1.0, ALU.add, ALU.mult, [B("sm")], [B("sm")])
            ts("dve", subg, cs("subln"), 1.0 - LAM_INIT, None, ALU.mult, None, [B("cst")], [B("subg")])
            negMa, negMb, sinkexp, neglam = sm[:, 4:5], sm[:, 5:6], sm[:, 8:16], sm[:, 21:22]

            m1 = A.mark()
            sinT = A.alloc([NT], F32, [B("sinT")])
            cosT = A.alloc([NT], F32, [B("cosT")])
            m2 = A.mark()
            posi = A.alloc([NT], I32, [B("posi")])
            ang = A.alloc([NT], F32, [B("ang")])
            ua = A.alloc([NT], F32, [B("ua")])
            ub = A.alloc([NT], F32, [B("ub")])
            dma("sp", posi, pos_d, [], [B("posi")])
            cp("dve", ua, posi, [B("posi")], [B("ua")])
            ts("dve", ang, ua, cs("invf"), None, ALU.mult, None, [B("ua"), B("cst")], [B("ang")])
            for tab, tb, off in ((sinT, B("sinT"), 0.0), (cosT, B("cosT"), 0.25)):
                ts("dve", ua, ang, 1.0 / TWO_PI, off, ALU.mult, ALU.add, [B("ang")], [B("ua")])
                ts("dve", ub, ua, MAGIC, None, ALU.add, None, [B("ua")], [B("ub")])
                ts("dve", ua, ub, -MAGIC, None, ALU.add, None, [B("ub")], [B("ua")])
                stt(ub, ua, -C1, ang, ALU.mult, ALU.add, [B("ua"), B("ang")], [B("ub")])
                stt(ub, ua, -C2, ub, ALU.mult, ALU.add, [B("ua"), B("ub")], [B("ub")])
                ts("dve", ub, ub, off * TWO_PI, PI_CL, ALU.add, ALU.min, [B("ub")], [B("ub")])
                ts("dve", ub, ub, -PI_CL, None, ALU.max, None, [B("ub")], [B("ub")])
                act(tab, ub, AF.Sin, [B("ub")], [tb])
            A.release(m2)
            wq = [A.alloc([8, 128], BF16, [B("wq", i_, 0), B("wq", i_, 1)]) for i_ in range(3)]
            qg = [A.alloc([GT], F32, [B("qg", i_)]) for i_ in range(2)]
            sqq = [A.alloc([GT], F32, [B("sqq", i_)]) for i_ in range(2)]
            rs = [A.alloc([GT], F32, [B("rs", i_)]) for i_ in range(2)]
            t1 = [A.alloc([GT], F32, [B("t1", i_)]) for i_ in range(2)]
            t2 = [A.alloc([GT], F32, [B("t2", i_)]) for i_ in range(2)]
            kst = [A.alloc([GT], BF16, [B("kst", i_)]) for i_ in range(2)]
            chunks = []
            for m_ in range(4):
                chunks.append(("q", OQA + m_ * 128, 0, m_))
            for g in range(2):
                chunks.append(("kd", OKA + g * 64, 1, 4 + g))
            for h in range(4):
                chunks.append(("q", OQB + h * 128, 2, 4 + h))
            for h in range(4):
                chunks.append(("k", OKB + h * 128, 3, h))
            nkc = [0]
            pitems = [(ci, G) for ci in range(len(chunks)) for G in range(NG)]

            def stage_a(n):
                ci, G = pitems[n]
                kind, col, gi, di = chunks[ci]
                w_ = wq[ci % 3]
                wb0, wb1 = B("wq", ci % 3, 0), B("wq", ci % 3, 1)
                if G == 0:
                    if kind == "kd":
                        dma("pool", w_[:, :, 0:64], win_d[:, col:col + 64].rearrange("(k p) n -> p k n", p=128), [], [wb0])
                        dma("pool", w_[:, :, 64:128], win_d[:, col:col + 64].rearrange("(k p) n -> p k n", p=128), [], [wb1])
                    else:
                        dma("pool", w_, win_d[:, col:col + 128].rearrange("(k p) n -> p k n", p=128), [], [wb0, wb1])
                gs = slice(G * GT, (G + 1) * GT)
                i2 = n % 2
                pq = PS[1 + i2]
                for k in range(8):
                    mm(pq, w_[:, k, :], hT[:, k, gs], k == 0, k == 7, [wb0, wb1, B("hT", k, G)], [PB[1 + i2]])
                act(qg[i2], pq, AF.Identity, [PB[1 + i2], B("cst")], [B("qg", i2)], scale=cs("qkg", gi, gi + 1))
                act(sqq[i2], pq, AF.Square, [PB[1 + i2]], [B("sqq", i2)])

            def stage_b(n):
                ci, G = pitems[n]
                kind, col, gi, di = chunks[ci]
                gs = slice(G * GT, (G + 1) * GT)
                i2 = n % 2
                pst, pr = PS[3 + i2], PS[5 + i2]
                mm(pst, cs("bones"), sqq[i2], True, True, [B("sqq", i2), B("cst")], [PB[3 + i2]])
                mm(pr, cs("rotm"), qg[i2], True, True, [B("qg", i2), B("cst")], [PB[5 + i2]])
                act(rs[i2], pst, AF.Ln, [PB[3 + i2], B("misc")], [B("rs", i2)], bias=misc[:, 0:1], scale=1.0 / 64)
                act(rs[i2], rs[i2], AF.Exp, [B("rs", i2)], [B("rs", i2)], scale=-0.5)
                tt("dve", t1[i2], qg[i2], cosT[:, gs], ALU.mult, [B("qg", i2), B("cosT")], [B("t1", i2)])
                tt("dve", t2[i2], pr, sinT[:, gs], ALU.mult, [PB[5 + i2], B("sinT")], [B("t2", i2)])
                tt("pool", t1[i2], t1[i2], t2[i2], ALU.add, [B("t1", i2), B("t2", i2)], [B("t1", i2)])
                if kind == "q":
                    tt("dve", QT[:, di, gs], t1[i2], rs[i2], ALU.mult, [B("t1", i2), B("rs", i2)], QB(di, G))
                else:
                    nk = nkc[0]
                    ks = kst[nk % 2]
                    tt("dve", ks, t1[i2], rs[i2], ALU.mult, [B("t1", i2), B("rs", i2)], [B("kst", nk % 2)])
                    dma("sp", kgi[di][:, gs], ks, [B("kst", nk % 2)], [B("kgi", di, G)])
                    nkc[0] += 1

            stage_a(0)
            for n in range(len(pitems)):
                if n + 1 < len(pitems):
                    stage_a(n + 1)
                stage_b(n)
            dump("qa0", QT[:, 0, 0:512], 512, QB(0, 0))
            dump("qb0", QT[:, 4, 0:512], 512, QB(4, 0))
            dump("qa3l", QT[:, 3, 1536:2048], 512, QB(3, 3))
            dump("sinT", sinT[:, 1536:2048], 512, [B("sinT")])
            dump("cosT", cosT[:, 1536:2048], 512, [B("cosT")])
            A.release(m1)

            if SUB == 1:
                A.release(mM)
                return
            m1 = A.mark()
            wv = A.alloc([8, 640], BF16, [B("wv", 0), B("wv", 1)])
            Vst = A.alloc([4, 2112], BF16, [B("Vst")])
            Vst4 = Vst.rearrange("p h (t e) -> p h t e", e=132)
            dma("pool", wv[:, :, 0:512], win_d[:, OVB:OVB + 512].rearrange("(k p) n -> p k n", p=128), [], [B("wv", 0)])
            dma("pool", wv[:, :, 512:640], win_d[:, OVA:OVA + 128].rearrange("(k p) n -> p k n", p=128), [], [B("wv", 1)])
            S.add("pool", lambda e: e.memset(Vst, 1.0), [], [B("Vst")])
            S.add("pool", lambda e: e.memset(Vast, 1.0), [], [B("Vast")])
            for t_ in range(16):
                G = t_ // 4
                i2 = t_ % 2
                for k in range(8):
                    mm(PS[1 + i2], hT[:, k, t_ * 128:(t_ + 1) * 128], wv[:, k, 0:512], k == 0, k == 7,
                       [B("wv", 0), B("hT", k, G)], [PB[1 + i2]])
                for k in range(8):
                    mm(PS[3 + i2][:, 0:128], hT[:, k, t_ * 128:(t_ + 1) * 128], wv[:, k, 512:640], k == 0, k == 7,
                       [B("wv", 1), B("hT", k, G)], [PB[3 + i2]])
                cp("act", Vst4[:, :, t_, 0:128], PS[1 + i2].rearrange("p (h e) -> p h e", e=128), [PB[1 + i2]], [B("Vst")])
                cp("dve", Vast4[:, t_, :, 0:64], PS[3 + i2][:, 0:128].rearrange("p (g e) -> p g e", e=64),
                   [PB[3 + i2]], [B("Vast")])
            for h in range(4):
                dma("sp", vgi[h], Vst[:, h, :], [B("Vst")], [B("vgi", h)])
            dma("sp", vgi[4], Vast, [B("Vast")], [B("vgi", 4)])
            A.release(m1)

            if SUB == 2:
                A.release(mM)
                return
            groups = [[0, 1, 2, 3], [4, 5, 6, 7]]
            def allgather(i_ap, o_ap, rd, wr):
                if NOCC:
                    for rp_ in range(4):
                        dma("sp", o_ap[rp_ * 128:(rp_ + 1) * 128, :], i_ap, rd, [B("nocc", wr[0].name, rp_)])
                    S.add("sp", lambda e: e.nop(), [B("nocc", wr[0].name, rp_) for rp_ in range(4)], wr)
                    return
                S.add("pool", lambda e: e.collective_compute("AllGather", ALU.bypass, replica_groups=groups,
                                                              ins=[i_ap], outs=[o_ap]), rd, wr, kind="cc")
            for di in (4, 5, 0, 1, 2, 3):
                allgather(kgi[di], kgo[di], [B("kgi", di, G) for G in range(NG)], [B("kgo", di)])
            for h in (4, 0, 1, 2, 3):
                allgather(vgi[h], vgo[h], [B("vgi", h)], [B("vgo", h)])

            m1 = A.mark()
            KAl = A.alloc([2, NT], BF16, [B("KAl")])
            khalo = A.alloc([2, 8, 128], BF16, [B("khalo")])
            vhalo = A.alloc([8, 132], BF16, [B("vhalo")])
            vhalo4 = vhalo.rearrange("p s (g e) -> p s g e", e=66)
            NEW = 4
            ew = [A.alloc([GT], BF16, [B("ew", i_)]) for i_ in range(NEW)]
            den = A.alloc([8], F32, [B("den")])
            oat = [A.alloc([256], BF16, [B("oat", i_)]) for i_ in range(2)]
            for g in range(2):
                dma("sp", KAl[:, g, :], kgi[4 + g], [B("kgi", 4 + g, G) for G in range(NG)], [B("KAl")])
            for rp in range(4):
                for ed in range(2):
                    slot = rp * 2 + ed
                    c0 = 0 if ed == 0 else 1920
                    kc = 0 if ed == 0 else 15
                    for g in range(2):
                        dma("sp", khalo[:, g, slot, :], kgo[4 + g][rp * 128:(rp + 1) * 128, c0:c0 + 128],
                            [B("kgo", 4 + g)], [B("khalo")])
                    dma("sp", vhalo[:, slot, :], vgo[4][rp * 128:(rp + 1) * 128, kc * 132:(kc + 1) * 132],
                        [B("vgo", 4)], [B("vhalo")])
            PSb6 = PSD[3][:, 0:512].bitcast(BF16)
            wun = []
            for j in list(range(1, 15)) + [0, 15]:
                for g in range(2):
                    units = []
                    for jj in (j - 1, j, j + 1):
                        if 0 <= jj <= 15:
                            mk_ = None if jj == j else (("m", mk("mprev")) if jj == j - 1 else ("m", mk("mnext")))
                            units.append((KAl[:, g, jj * 128:(jj + 1) * 128], B("KAl"), Vast4[:, jj, g, 0:65], B("Vast"), mk_))
                    if j == 0 or j == 15:
                        for slot in range(8):
                            so = slot if j == 0 else 8 + slot
                            mk_ = ("s", cs("sel", so, so + 1), mk("mprev") if j == 0 else mk("mnext"))
                            units.append((khalo[:, g, slot, :], B("khalo"), vhalo4[:, slot, g, 0:65], B("vhalo"), mk_))
                    for u, un in enumerate(units):
                        wun.append((j, g, u, len(units)) + un)

            def w_qk(n):
                j, g, u, nu, kap, kb_, vap, vb_, mk_ = wun[n]
                pw = n % 2
                pwb = [PB[2 * pw], PB[2 * pw + 1]]
                for hh in range(4):
                    head = 4 * g + hh
                    ch, half = head // 2, head % 2
                    mm(PS[2 * pw + half][:, (hh // 2) * 128:(hh // 2 + 1) * 128], kap[half * 64:(half + 1) * 64, :],
                       QT[half * 64:(half + 1) * 64, ch, j * 128:(j + 1) * 128], True, True,
                       [kb_, B("QT", ch, j)], [pwb[half]])
                e_ = ew[n % NEW]
                eb = B("ew", n % NEW)
                act(e_.rearrange("p (a b) -> p a b", b=256), PSD[pw].rearrange("p (a b) -> p a b", b=512)[:, :, 0:256],
                    AF.Exp, pwb + [B("sm")], [eb], bias=negMa, scale=0.125)
                if mk_ is not None:
                    if mk_[0] == "m":
                        tt("dve", e_, e_, mk_[1], ALU.mult, [eb, B("msk")], [eb])
                    else:
                        stt(e_, e_, mk_[1], mk_[2], ALU.mult, ALU.mult, [eb, B("cst"), B("msk")], [eb])

            pcc = [0]

            def w_pv(n):
                j, g, u, nu, kap, kb_, vap, vb_, mk_ = wun[n]
                pc = pcc[0]
                pso = PS[4 + pc % 2]
                pob = PB[4 + pc % 2]
                e_ = ew[n % NEW]
                eb = B("ew", n % NEW)
                for hh in range(4):
                    ec_ = (hh % 2) * 256 + (hh // 2) * 128
                    mm(pso[:, hh * 66:hh * 66 + 65], e_[:, ec_:ec_ + 128], vap,
                       (u == 0 and hh == 0), u == nu - 1, [eb, vb_], [pob], skip=True)
                if u != nu - 1:
                    return
                ot = oat[pc % 2]
                otb = B("oat", pc % 2)
                for hh in range(4):
                    ts("dve", den[:, hh:hh + 1], pso[:, hh * 66 + 64:hh * 66 + 65], sinkexp[:, 4 * g + hh:4 * g + hh + 1],
                       None, ALU.add, None, [pob, B("sm")], [B("den")])
                S.add("dve", lambda e, o_=den[:, 4:8], i_=den[:, 0:4]: e.reciprocal(out=o_, in_=i_), [B("den")], [B("den")])
                for hh in range(4):
                    act(ot[:, hh * 64:(hh + 1) * 64], pso[:, hh * 66:hh * 66 + 64], AF.Identity, [pob, B("den")], [otb],
                        scale=den[:, 4 + hh:5 + hh])
                for i_ in range(2):
                    tr(PSb6[:, i_ * 128:(i_ + 1) * 128], ot[:, i_ * 128:(i_ + 1) * 128], identb, [otb, B("msk")], [PB[6]])
                for i_ in range(2):
                    cp("dve", QT[:, 2 * g + i_, j * 128:(j + 1) * 128], PSb6[:, i_ * 128:(i_ + 1) * 128], [PB[6]],
                       [B("QT", 2 * g + i_, j)])
                pcc[0] += 1

            w_qk(0)
            for n in range(len(wun)):
                if n + 1 < len(wun):
                    w_qk(n + 1)
                w_pv(n)
            dump("oa0", QT[:, 0, 0:512], 512, QB(0, 0))
            dump("oa3l", QT[:, 3, 1536:2048], 512, QB(3, 3))
            A.release(m1)

            if SUB == 3:
                A.release(mM)
                return
            m1 = A.mark()
            NKV = 3
            Kp = [A.alloc([NT], BF16, [B("Kp", i_)]) for i_ in range(NKV)]
            Vp = [A.alloc([16, 132], BF16, [B("Vp", i_)]) for i_ in range(NKV)]
            NE = 4
            e2 = [A.alloc([1024], BF16, [B("e2", i_)]) for i_ in range(NE)]
            ESP = 768
            esum = A.alloc([1024], F32, [B("esum", 0), B("esum", 1)])
            frc = [A.alloc([GT], F32, [B("frc", i_)]) for i_ in range(2)]
            fu = A.alloc([GT], F32, [B("fu")])
            fv = A.alloc([GT], F32, [B("fv")])
            sublnT = A.alloc([1], F32, [B("sublnT")])
            tt("dve", fu[:, 0:128], subg, ident, ALU.mult, [B("subg"), B("cst")], [B("fu")])
            S.add("dve", lambda e: e.tensor_reduce(out=sublnT, in_=fu[:, 0:128], axis=AX, op=ALU.add), [B("fu")], [B("sublnT")])
            fin = A.alloc([32], F32, [B("fin", i_) for i_ in range(4)])
            ftmp = A.alloc([128], F32, [B("ftmp")])
            fob4 = A.alloc([4, 128], F32, [B("fob", i_) for i_ in range(4)])
            fjunk = A.alloc([128], F32, [B("fjunk")])
            obn = [A.alloc([128], BF16, [B("obn", i_)]) for i_ in range(2)]
            PSb7 = PSD[3][:, 512:1024].bitcast(BF16)
            pcn = 0
            ec = 0
            fc = 0

            def acc(t, qq):
                idx = t * 4 + qq
                return PS[4 + idx // 3][:, (idx % 3) * 130:(idx % 3) * 130 + 129], PB[4 + idx // 3], idx % 3 == 0

            for h in range(4):
                for G in range(NG):
                    gs = slice(G * GT, (G + 1) * GT)
                    items = [(rp, kc) for rp in range(4) for kc in range(16)]
                    st_ = {}

                    def issue_qk(i, h=h, G=G, gs=gs, st_=st_):
                        nonlocal pcn, ec
                        rp, kc = items[i]
                        if kc == 0:
                            kp, vp = Kp[pcn % NKV], Vp[pcn % NKV]
                            kpb, vpb = B("Kp", pcn % NKV), B("Vp", pcn % NKV)
                            dma("sp", kp, kgo[h][rp * 128:(rp + 1) * 128, :], [B("kgo", h)], [kpb])
                            dma("sp", vp, vgo[h][rp * 128:(rp + 1) * 128, :].rearrange("p (t e) -> p t e", e=132),
                                [B("vgo", h)], [vpb])
                            st_["piece", rp] = (kp, vp, kpb, vpb)
                            pcn += 1
                        kp, vp, kpb, vpb = st_["piece", rp]
                        dpi = ec % 2
                        e_ = e2[ec % NE]
                        eb = B("e2", ec % NE)
                        st_["e", i] = (e_, eb)
                        ec += 1
                        mm(PS[2 * dpi], kp[0:64, kc * 128:(kc + 1) * 128], QT[0:64, 4 + h, gs], True, True,
                           [kpb] + QB(4 + h, G), [PB[2 * dpi]])
                        mm(PS[2 * dpi + 1], kp[64:128, kc * 128:(kc + 1) * 128], QT[64:128, 4 + h, gs], True, True,
                           [kpb] + QB(4 + h, G), [PB[2 * dpi + 1]])
                        act(e_, PSD[dpi][:, :], AF.Exp, [PB[2 * dpi], PB[2 * dpi + 1], B("sm")], [eb], bias=negMb, scale=0.125)

                    def issue_pv(i, st_=st_, h=h, G=G):
                        rp, kc = items[i]
                        kp, vp, kpb, vpb = st_["piece", rp]
                        e_, eb = st_["e", i]
                        last = i == len(items) - 1
                        for t in range(2):
                            mm(PS[4 + t], vp[:, kc, 0:128], e_[:, t * 512:(t + 1) * 512], i == 0, last, [eb, vpb], [PB[4 + t]])
                        if i == 0:
                            cp("dve", esum[:, 0:ESP], e_[:, 0:ESP], [eb], [B("esum", 0)])
                            cp("pool", esum[:, ESP:1024], e_[:, ESP:1024], [eb], [B("esum", 1)])
                        else:
                            tt("dve", esum[:, 0:ESP], esum[:, 0:ESP], e_[:, 0:ESP], ALU.add, [eb, B("esum", 0)], [B("esum", 0)])
                            tt("pool", esum[:, ESP:1024], esum[:, ESP:1024], e_[:, ESP:1024], ALU.add, [eb, B("esum", 1)],
                               [B("esum", 1)])

                    issue_qk(0)
                    issue_qk(1)
                    for i in range(len(items)):
                        if i + 2 < len(items):
                            issue_qk(i + 2)
                        issue_pv(i)
                    esb = [B("esum", 0), B("esum", 1)]
                    for t in range(2):
                        mm(PS[6 + t], cs("ones"), esum[:, t * 512:(t + 1) * 512], True, True, esb + [B("cst")], [PB[6 + t]])
                        act(frc[t], PS[6 + t], AF.Ln, [PB[6 + t]], [B("frc", t)])
                        act(frc[t], frc[t], AF.Exp, [B("frc", t)], [B("frc", t)], scale=-1.0)
                    tt("dve", fu, PS[5], frc[1], ALU.mult, [PB[5], B("frc", 1)], [B("fu")])
                    tt("dve", fv, PS[4], frc[0], ALU.mult, [PB[4], B("frc", 0)], [B("fv")])
                    stt(fv, fu, neglam, fv, ALU.mult, ALU.add, [B("fu"), B("fv"), B("sm")], [B("fv")])
                    act(fu, fv, AF.Square, [B("fv")], [B("fu")])
                    mm(PS[6], cs("ones"), fu, True, True, [B("fu"), B("cst")], [PB[6]])
                    act(frc[0], PS[6], AF.Ln, [PB[6], B("misc")], [B("frc", 0)], bias=misc[:, 0:1], scale=1.0 / 128)
                    act(frc[0], frc[0], AF.Exp, [B("frc", 0)], [B("frc", 0)], scale=-0.5)
                    stt(QT[:, 4 + h, gs], fv, sublnT, frc[0], ALU.mult, ALU.mult, [B("fv"), B("frc", 0), B("sublnT")], QB(4 + h, G))
            dump("ob0", QT[:, 4, 0:512], 512, QB(4, 0))
            dump("ob3l", QT[:, 7, 1536:2048], 512, QB(7, 3))
            A.release(m1)

            if SUB == 4:
                A.release(mM)
                return
            m1 = A.mark()
            NW = 2
            wsl = [(A.alloc([8, 128], BF16, [B("wga", i_)]), A.alloc([8, 128], BF16, [B("wgb", i_)]),
                    A.alloc([4, 128], BF16, [B("wba", i_)]), A.alloc([4, 128], BF16, [B("wbb", i_)]),
                    A.alloc([1024], BF16, [B("wo2", i_)])) for i_ in range(NW)]
            sga = [A.alloc([GT], F32, [B("sga", i_)]) for i_ in range(2)]
            sgb = [A.alloc([GT], F32, [B("sgb", i_)]) for i_ in range(2)]
            mT = [A.alloc([GT], BF16, [B("mT", i_)]) for i_ in range(2)]
            zcc = [0]
            mitems = [(c1, G) for c1 in range(8) for G in range(NG)]

            def stage_g(q):
                c1, G = mitems[q]
                wga, wgb, wba, wbb, wo2 = wsl[c1 % NW]
                bs = [B(nm, c1 % NW) for nm in ("wga", "wgb", "wba", "wbb", "wo2")]
                cols = slice(c1 * 128, (c1 + 1) * 128)
                if G == 0:
                    dma("pool", wga, win_d[:, OGA + c1 * 128:OGA + (c1 + 1) * 128].rearrange("(k p) n -> p k n", p=128), [], [bs[0]])
                    dma("pool", wgb, win_d[:, OGB + c1 * 128:OGB + (c1 + 1) * 128].rearrange("(k p) n -> p k n", p=128), [], [bs[1]])
                    dma("pool", wba, wa_d[:, cols].rearrange("(k p) n -> p k n", p=128), [], [bs[2]])
                    dma("pool", wbb, wb_d[:, cols].rearrange("(k p) n -> p k n", p=128), [], [bs[3]])
                    dma("pool", wo2, wo_d[c1 * 128:(c1 + 1) * 128, :], [], [bs[4]])
                gs = slice(G * GT, (G + 1) * GT)
                i2 = q % 2
                for k in range(8):
                    mm(PS[0], wga[:, k, :], hT[:, k, gs], k == 0, k == 7, [bs[0], B("hT", k, G)], [PB[0]])
                for k in range(8):
                    mm(PS[1], wgb[:, k, :], hT[:, k, gs], k == 0, k == 7, [bs[1], B("hT", k, G)], [PB[1]])
                for k in range(4):
                    mm(PS[2], wba[:, k, :], QT[:, k, gs], k == 0, k == 3, [bs[2]] + QB(k, G), [PB[2]])
                for k in range(4):
                    mm(PS[3], wbb[:, k, :], QT[:, 4 + k, gs], k == 0, k == 3, [bs[3]] + QB(4 + k, G), [PB[3]])
                act(sga[i2], PS[0], AF.Sigmoid, [PB[0]], [B("sga", i2)])
                act(sgb[i2], PS[1], AF.Sigmoid, [PB[1]], [B("sgb", i2)])
                tt("dve", sga[i2], sga[i2], PS[2], ALU.mult, [B("sga", i2), PB[2]], [B("sga", i2)])
                tt("dve", sgb[i2], sgb[i2], PS[3], ALU.mult, [B("sgb", i2), PB[3]], [B("sgb", i2)])
                tt("pool", mT[i2], sga[i2], sgb[i2], ALU.add, [B("sga", i2), B("sgb", i2)], [B("mT", i2)])

            def stage_z(q):
                c1, G = mitems[q]
                wo2 = wsl[c1 % NW][4]
                bwo = B("wo2", c1 % NW)
                gs = slice(G * GT, (G + 1) * GT)
                i2 = q % 2
                for c2 in range(8):
                    pz = 4 + zcc[0] % 4
                    mm(PS[pz], wo2[:, c2 * 128:(c2 + 1) * 128], mT[i2], True, True, [bwo, B("mT", i2)], [PB[pz]])
                    stt(xT[:, c2, gs], PS[pz], scal[:, 24 + 16 + c2:24 + 17 + c2], xT[:, c2, gs], ALU.mult, ALU.add,
                        [PB[pz], B("scal", 1), B("xT", c2, G)], [B("xT", c2, G)])
                    zcc[0] += 1

            stage_g(0)
            for q in range(len(mitems)):
                if q + 1 < len(mitems):
                    stage_g(q + 1)
                stage_z(q)
            A.release(m1)
            A.release(mM)

        if stage >= 2:
            mixer()

        if stage >= 3:
            norm_mod(2)
            ffn(2, w2i_d, w2o_d)

        m0 = A.mark()
        ost = [A.alloc([D], F32, [B("ost", i_)]) for i_ in range(2)]
        outs = []
        for t_ in range(16):
            G = t_ // 4
            os_ = ost[t_ % 2]
            for c in range(8):
                pb = 2 * (t_ % 2) + c // 4
                tr(PS[pb][:, (c % 4) * 128:(c % 4 + 1) * 128], xT[:, c, t_ * 128:(t_ + 1) * 128], ident,
                   [B("xT", c, G), B("cst")], [PB[pb]])
            for hf in range(2):
                pb = 2 * (t_ % 2) + hf
                cp("act" if hf else "dve", os_[:, hf * 512:(hf + 1) * 512], PS[pb][:, :], [PB[pb]], [B("ost", t_ % 2)])
            outs.append(dma("sp", out_d[t_ * 128:(t_ + 1) * 128, :], os_, [B("ost", t_ % 2)], [B("out")]))
        A.release(m0)
        S.emit(st, final_ops=outs + dbg_state["ops"])
    nc._dbg_names = dbg_state["names"]
    return nc


def _host_consts(r, inputs):
    b = r // 4
    f32 = np.float32
    cst = np.zeros((128, NCC), f32)

    def put(name, arr):
        lo, hi = CC[name]
        cst[:, lo:hi] = arr

    put("cT", inputs["c"][b].reshape(8, 128).T)
    put("bmod", inputs["b_mod"][0].reshape(72, 128).T)
    put("gains", np.concatenate([inputs[n][0].reshape(8, 128).T for n in ("norm_ffn1", "norm_mix", "norm_ffn2")], 1))
    put("qkg", np.stack([np.tile(inputs[n][0], 2) for n in ("qn_a", "kn_a", "qn_b", "kn_b")], 1))
    put("qkgb", np.broadcast_to(np.concatenate([inputs[n][0] for n in ("qn_a", "kn_a", "qn_b", "kn_b")])[None], (128, 256)))
    put("sink", np.broadcast_to(inputs["sink_a"][0][None], (128, 8)))
    put("lam", np.broadcast_to(np.concatenate([inputs[n][0] for n in ("lam_q1", "lam_k1", "lam_q2", "lam_k2")])[None], (128, 256)))
    put("subln", np.broadcast_to(inputs["subln_b"][0][None], (128, 128)))
    invf = (10000.0 ** (-np.arange(0, 64, 2, dtype=f32) / f32(64))).astype(f32)
    put("invf", np.tile(invf, 4)[:, None])
    put("ident", np.eye(128, dtype=f32))
    put("ones", np.ones((128, 128), f32))
    bo = np.zeros((128, 128), f32)
    bo[:64, :64] = 1
    bo[64:, 64:] = 1
    put("bones", bo)
    rot = np.zeros((128, 128), f32)
    for blk in range(2):
        for d in range(64):
            if d < 32:
                rot[blk * 64 + d + 32, blk * 64 + d] = -1.0
            else:
                rot[blk * 64 + d - 32, blk * 64 + d] = 1.0
    put("rotm", rot)
    msk = np.zeros((128, NMB), f32)
    msk[:, 0:128] = np.eye(128)
    kk = np.arange(128)[:, None]
    qq = np.arange(128)[None, :]
    mprev = (kk >= qq).astype(f32)
    mnext = (kk <= qq).astype(f32)
    msk[:, 128:640] = np.tile(mprev, (1, 4))
    msk[:, 640:1152] = np.tile(mnext, (1, 4))
    rr = r % 4
    sel = np.zeros((128, 16), f32)
    if rr > 0:
        sel[:, (rr - 1) * 2 + 1] = 1.0
    if rr < 3:
        sel[:, 8 + (rr + 1) * 2 + 0] = 1.0
    put("sel", sel)
    return cst, msk.astype(ml_dtypes.bfloat16)


_STAGE = 3


def kernel(**inputs):
    inputs = {k: np.asarray(v) for k, v in inputs.items()}
    nc = build(_STAGE)
    in_maps = []
    for r in range(8):
        b, t0 = r // 4, (r % 4) * NT
        cst, msk = _host_consts(r, inputs)
        in_maps.append({
            "x": np.ascontiguousarray(inputs["x"][b, t0:t0 + NT, :]),
            "pos": np.ascontiguousarray(np.broadcast_to(inputs["positions"][b, t0:t0 + NT][None, :], (128, NT))).astype(np.int32),
            "cst": cst, "msk": msk,
            "w_mod": inputs["w_mod"][0], "w_ffn1_in": inputs["w_ffn1_in"][0], "w_ffn1_out": inputs["w_ffn1_out"][0],
            "w_in": inputs["w_in"][0], "w_branch_a": inputs["w_branch_a"][0], "w_branch_b": inputs["w_branch_b"][0],
            "w_out": inputs["w_out"][0], "w_ffn2_in": inputs["w_ffn2_in"][0], "w_ffn2_out": inputs["w_ffn2_out"][0],
        })
    res = run_bass_kernel_spmd(nc, in_maps, core_ids=list(range(8)))
    out = np.empty((2, 8192, D), np.float32)
    for r in range(8):
        b, t0 = r // 4, (r % 4) * NT
        out[b, t0:t0 + NT, :] = np.asarray(res.results[r]["out"])
    return out
```

```python
import math
import numpy as np
import ml_dtypes
from contextlib import ExitStack
import concourse.bass as bass
import concourse.mybir as mybir
from concourse.bass_utils import run_bass_kernel_spmd

F32 = mybir.dt.float32
BF16 = mybir.dt.bfloat16
I32 = mybir.dt.int32
AF = mybir.ActivationFunctionType
ALU = mybir.AluOpType

ENGS = ("pe", "act", "dve", "pool", "sp")

D = 1024
NT = 2048
NG = 4
GT = 512
DFF = 2816
NHC = 22
EPS = 1e-6
LAM_INIT = 0.8 - 0.6 * math.exp(-0.3 * 0)
MAGIC = 12582912.0
TWO_PI = 2.0 * math.pi
C1 = 6.28125
C2 = TWO_PI - C1
PI_CL = 3.1415925

OQA, OKA, OVA, OQB, OKB, OVB, OGA, OGB = 0, 512, 640, 768, 1280, 1792, 2304, 3328

CC = {}
_o = 0
for _n, _w in [("cT", 8), ("bmod", 72), ("gains", 24), ("qkg", 4), ("qkgb", 256), ("sink", 8), ("lam", 256),
               ("subln", 128), ("invf", 1), ("sel", 16), ("ident", 128), ("ones", 128), ("bones", 128), ("rotm", 128)]:
    CC[_n] = (_o, _o + _w)
    _o += _w
NCC = _o
MB = {"identb": (0, 128), "mprev": (128, 640), "mnext": (640, 1152)}
NMB = 1152


class Buf:
    __slots__ = ("name",)

    def __init__(self, name):
        self.name = name


class Op:
    __slots__ = ("eng", "fn", "deps", "sig", "sem", "val", "kind", "name")


class Sched:
    NDMA = 16

    def __init__(self, nc):
        self.nc = nc
        self.ops = {e: [] for e in ENGS}
        self.last_w = {}
        self.readers = {}

    def add(self, eng, fn, reads=(), writes=(), kind="c", name=""):
        op = Op()
        op.eng, op.fn, op.kind, op.name = eng, fn, kind, name
        op.sig = False
        op.sem = None
        op.val = 0
        deps = []
        seen = set()

        def adddep(d):
            if d is None or d is op or id(d) in seen:
                return
            if eng == "pe" and d.eng == "pe" and d.kind == "c" and kind == "c":
                return
            seen.add(id(d))
            deps.append(d)

        for b in reads:
            adddep(self.last_w.get(b))
        for b in writes:
            adddep(self.last_w.get(b))
            for r in self.readers.get(b, ()):
                adddep(r)
        op.deps = deps
        for d in deps:
            d.sig = True
        for b in reads:
            self.readers.setdefault(b, []).append(op)
        for b in writes:
            self.last_w[b] = op
            self.readers[b] = []
        self.ops[eng].append(op)
        return op

    def inherit(self, new, olds):
        lst = self.readers.setdefault(new, [])
        for o in olds:
            w = self.last_w.get(o)
            if w is not None:
                lst.append(w)
            lst.extend(self.readers.get(o, ()))

    def emit(self, stack, final_ops=()):
        nc = self.nc
        EPOCH = 1500
        nsig = {e: sum(1 for o in self.ops[e] if o.sig and o.kind == "c") for e in ENGS}
        sems = {e: [stack.enter_context(nc.semaphore(f"s_{e}{i}")) for i in range(nsig[e] // EPOCH + 1)] for e in ENGS}
        ccsem = stack.enter_context(nc.semaphore("s_cc"))
        dsems = {e: [stack.enter_context(nc.semaphore(f"d_{e}{i}")) for i in range(self.NDMA)]
                 for e in ("sp", "pool", "act")}
        cnt = {e: 0 for e in ENGS}
        dcnt = {e: [0] * self.NDMA for e in dsems}
        drot = {e: 0 for e in dsems}
        cccnt = 0
        for e in ENGS:
            for op in self.ops[e]:
                if op.kind == "dma":
                    i = drot[e] % self.NDMA
                    drot[e] += 1
                    dcnt[e][i] += 16
                    op.sem, op.val, op.sig = dsems[e][i], dcnt[e][i], True
                    op.name = dcnt[e][i] - 16
                elif op.kind == "cc":
                    cccnt += 1
                    op.sem, op.val, op.sig = ccsem, cccnt, True
                elif op.sig:
                    op.sem, op.val = sems[e][cnt[e] // EPOCH], cnt[e] % EPOCH + 1
                    cnt[e] += 1
        block = stack.enter_context(nc.Block())
        handles = {"pe": block.tensor, "act": block.scalar, "dve": block.vector,
                   "pool": block.gpsimd, "sp": block.sync}

        def make(e):
            def body(eng):
                waited = {}
                for op in self.ops[e]:
                    for d in op.deps:
                        k = id(d.sem)
                        if waited.get(k, 0) >= d.val:
                            continue
                        eng.wait_ge(d.sem, d.val)
                        waited[k] = d.val
                    if op.kind == "dma" and op.name and waited.get(id(op.sem), 0) < op.name:
                        eng.wait_ge(op.sem, op.name)
                        waited[id(op.sem)] = op.name
                    inst = op.fn(eng)
                    if op.sig:
                        inst.then_inc(op.sem, 16 if op.kind == "dma" else 1)
                if e == "sp":
                    for q_ in dsems:
                        for i_, sm_ in enumerate(dsems[q_]):
                            if dcnt[q_][i_] > waited.get(id(sm_), 0):
                                eng.wait_ge(sm_, dcnt[q_][i_])
                    if cccnt:
                        eng.wait_ge(ccsem, cccnt)
            return body

        for e in ENGS:
            handles[e](make(e))


class Arena:
    def __init__(self, t, nbytes, sched):
        self.t, self.n, self.top, self.S = t, nbytes, 0, sched
        self.live, self.dead = [], []

    def alloc(self, shape, dt, bufs):
        esz = 2 if dt == BF16 else 4
        ne = int(np.prod(shape))
        nb = (ne * esz + 63) // 64 * 64
        off = self.top
        self.top += nb
        assert self.top <= self.n, ("arena overflow", self.top, self.n)
        for (o_, e_, ob) in self.dead:
            if o_ < off + nb and off < e_:
                for b in bufs:
                    self.S.inherit(b, ob)
        self.live.append((off, off + nb, list(bufs)))
        v = self.t[:, off // 4: (off + nb) // 4]
        if dt != F32:
            v = v.bitcast(dt)
        v = v[:, 0:ne]
        if len(shape) == 2:
            v = v.rearrange("p (a b) -> p a b", b=shape[1])
        elif len(shape) == 3:
            v = v.rearrange("p (a b c) -> p a b c", b=shape[1], c=shape[2])
        return v

    def mark(self):
        return self.top

    def release(self, m):
        keep = []
        for a in self.live:
            (keep if a[0] < m else self.dead).append(a)
        self.live = keep
        self.top = m


DEBUG = False
SUB = 0
W2A = 0
W2P = 99
M1B = 0
NOCC = False
DBGW = 8192


def build(stage=3):
    nc = bass.Bass("TRN2", target_bir_lowering=False)
    dbg_d = nc.dram_tensor("dbg", [128, DBGW], F32, kind="ExternalOutput").ap() if DEBUG else None
    dbg_state = {"off": 0, "ops": [], "names": []}
    dt_in = lambda n, s, d=F32: nc.dram_tensor(n, s, d, kind="ExternalInput").ap()
    x_d = dt_in("x", [NT, D])
    pos_d = dt_in("pos", [128, NT], I32)
    cst_d = dt_in("cst", [128, NCC])
    msk_d = dt_in("msk", [128, NMB], BF16)
    wmod_d = dt_in("w_mod", [D, 9 * D])
    w1i_d = dt_in("w_ffn1_in", [D, 2 * DFF])
    w1o_d = dt_in("w_ffn1_out", [DFF, D])
    win_d = dt_in("w_in", [D, 4352])
    wa_d = dt_in("w_branch_a", [512, D])
    wb_d = dt_in("w_branch_b", [512, D])
    wo_d = dt_in("w_out", [D, D])
    w2i_d = dt_in("w_ffn2_in", [D, 2 * DFF])
    w2o_d = dt_in("w_ffn2_out", [DFF, D])
    out_d = nc.dram_tensor("out", [NT, D], F32, kind="ExternalOutput").ap()
    kgi = [nc.dram_tensor(f"kg_in{i}", [128, 2048], BF16, kind="Internal").ap() for i in range(6)]
    kgo = [nc.dram_tensor(f"kg_out{i}", [512, 2048], BF16, kind="Internal").ap() for i in range(6)]
    vgi = [nc.dram_tensor(f"vg_in{i}", [128, 2112], BF16, kind="Internal").ap() for i in range(5)]
    vgo = [nc.dram_tensor(f"vg_out{i}", [512, 2112], BF16, kind="Internal").ap() for i in range(5)]

    S = Sched(nc)
    bufs = {}

    def B(*k):
        if k not in bufs:
            bufs[k] = Buf(str(k))
        return bufs[k]

    with ExitStack() as st:
        ARENA_BYTES = 206 * 1024
        arena_t = st.enter_context(nc.sbuf_tensor("arena", [128, ARENA_BYTES // 4], F32))
        A = Arena(arena_t, ARENA_BYTES, S)
        PSD = [st.enter_context(nc.psum_tensor(f"psd{i}", [128, 1024], F32)) for i in range(4)]
        PS = [PSD[i // 2][:, (i % 2) * 512:(i % 2 + 1) * 512] for i in range(8)]
        PB = [B("ps", i) for i in range(8)]

        xT = A.alloc([8, NT], F32, [B("xT", c, G) for c in range(8) for G in range(NG)])
        hT = A.alloc([8, NT], BF16, [B("hT", c, G) for c in range(8) for G in range(NG)])
        cst = A.alloc([NCC], F32, [B("cst")])
        msk = A.alloc([NMB], BF16, [B("msk")])
        modT = A.alloc([72], F32, [B("modT", (0,)), B("modT", (1,)), B("modT", (2,))])
        scal = A.alloc([72], F32, [B("scal", 0), B("scal", 1), B("scal", 2)])
        misc = A.alloc([64], F32, [B("misc")])

        def cs(name, a=None, b=None):
            lo, hi = CC[name]
            if a is None:
                return cst[:, lo:hi]
            return cst[:, lo + a: lo + b]

        def mk(name, a=None, b=None):
            lo, hi = MB[name]
            if a is None:
                return msk[:, lo:hi]
            return msk[:, lo + a: lo + b]

        ident = cs("ident")
        identb = mk("identb")

        def dma(eng, out, in_, reads, writes):
            return S.add(eng, lambda e: e.dma_start(out=out, in_=in_), reads, writes, kind="dma")

        def mm(out, lhsT, rhs, start, stop, reads, writes, skip=False):
            if skip:
                return S.add("pe", lambda e: e.matmul(out, lhsT=lhsT, rhs=rhs, start=start, stop=stop,
                                                      skip_group_check=True), reads, writes)
            return S.add("pe", lambda e: e.matmul(out, lhsT=lhsT, rhs=rhs, start=start, stop=stop), reads, writes)

        def tr(out, in_, idn, reads, writes):
            return S.add("pe", lambda e: e.transpose(out, in_, idn), reads, writes)

        def act(out, in_, func, reads, writes, bias=None, scale=None):
            kw = {}
            if bias is not None:
                kw["bias"] = bias
            if scale is not None:
                kw["scale"] = scale
            return S.add("act", lambda e: e.activation(out=out, in_=in_, func=func, **kw), reads, writes)

        def tt(eng, out, in0, in1, op, reads, writes):
            return S.add(eng, lambda e: e.tensor_tensor(out=out, in0=in0, in1=in1, op=op), reads, writes)

        def ts(eng, out, in0, s1, s2, op0, op1, reads, writes):
            if op1 is None:
                return S.add(eng, lambda e: e.tensor_scalar(out=out, in0=in0, scalar1=s1, scalar2=None, op0=op0),
                             reads, writes)
            return S.add(eng, lambda e: e.tensor_scalar(out=out, in0=in0, scalar1=s1, scalar2=s2, op0=op0, op1=op1),
                         reads, writes)

        def stt(out, in0, scalar, in1, op0, op1, reads, writes):
            return S.add("dve", lambda e: e.scalar_tensor_tensor(out=out, in0=in0, scalar=scalar, in1=in1,
                                                                  op0=op0, op1=op1), reads, writes)

        def cp(eng, out, in_, reads, writes):
            if eng == "act":
                return S.add("act", lambda e: e.copy(out=out, in_=in_), reads, writes)
            return S.add(eng, lambda e: e.tensor_copy(out=out, in_=in_), reads, writes)

        dbg_st = A.alloc([512], F32, [B("dbgst")]) if DEBUG else None

        def dump(name, ap, w, reads):
            if not DEBUG:
                return
            cp("dve", dbg_st[:, 0:w], ap, reads, [B("dbgst")])
            o_ = dbg_state["off"]
            dbg_state["ops"].append(dma("sp", dbg_d[:, o_:o_ + w], dbg_st[:, 0:w], [B("dbgst")], [B("dbgout")]))
            dbg_state["names"].append((name, o_, w))
            dbg_state["off"] = o_ + w

        dma("sp", cst, cst_d, [], [B("cst")])
        dma("sp", msk, msk_d, [], [B("msk")])
        m0 = A.mark()
        cact = A.alloc([8], BF16, [B("cact")])
        act(cact, cs("cT"), AF.Silu, [B("cst")], [B("cact")])
        wm = [A.alloc([8, 1024], BF16, [B("wm", i_)]) for i_ in range(2)]

        def mod_part(vs, sub):
            for v in vs:
                wbuf = wm[v % 2]
                dma("pool", wbuf, wmod_d[:, v * 1024:(v + 1) * 1024].rearrange("(k p) n -> p k n", p=128),
                    [], [B("wm", v % 2)])
                for j in range(8):
                    for k in range(8):
                        mm(PS[7][:, v * 8 + j: v * 8 + j + 1], wbuf[:, k, j * 128:(j + 1) * 128], cact[:, k:k + 1],
                           k == 0, k == 7, [B("wm", v % 2), B("cact")], [PB[7]])
            lo, hi = vs[0] * 8, vs[-1] * 8 + 8
            tt("dve", modT[:, lo:hi], PS[7][:, lo:hi], cs("bmod", lo, hi), ALU.add, [PB[7], B("cst")], [B("modT", sub)])
            for i in sub:
                o = i * 24
                stt(scal[:, o:o + 8], modT[:, o + 8:o + 16], 1.0, cs("gains", i * 8, i * 8 + 8), ALU.add, ALU.mult,
                    [B("modT", sub), B("cst")], [B("scal", i)])
                cp("dve", scal[:, o + 8:o + 16], modT[:, o:o + 8], [B("modT", sub)], [B("scal", i)])
                ts("dve", scal[:, o + 16:o + 24], modT[:, o + 16:o + 24], 1.0 if i == 1 else 0.5, None, ALU.mult, None,
                   [B("modT", sub)], [B("scal", i)])

        mod_part([0, 1, 2], (0,))
        mM0 = m0

        m0 = A.mark()
        xst = [A.alloc([D], F32, [B("xst", i_)]) for i_ in range(3)]
        for t_ in range(16):
            xs = xst[t_ % 3]
            G = t_ // 4
            dma("sp", xs, x_d[t_ * 128:(t_ + 1) * 128, :], [], [B("xst", t_ % 3)])
            for c in range(8):
                tr(PS[c][:, (t_ % 4) * 128:(t_ % 4 + 1) * 128], xs[:, c * 128:(c + 1) * 128], ident,
                   [B("xst", t_ % 3), B("cst")], [PB[c]])
            if t_ % 4 == 3:
                for c in range(8):
                    cp("act" if c % 2 else "dve", xT[:, c, G * GT:(G + 1) * GT], PS[c][:, :], [PB[c]], [B("xT", c, G)])
        dump("xT00", xT[:, 0, 0:512], 512, [B("xT", 0, 0)])
        dump("xT73", xT[:, 7, 1536:2048], 512, [B("xT", 7, 3)])
        A.release(m0)

        def norm_mod(i):
            o = i * 24
            m0 = A.mark()
            sq = [A.alloc([GT], F32, [B("nsq", i_)]) for i_ in range(2)]
            lnv = A.alloc([GT], F32, [B("nln")])
            rstd = A.alloc([GT], F32, [B("nrstd")])
            tmp = [A.alloc([GT], F32, [B("ntmp", i_)]) for i_ in range(2)]
            for G in range(NG):
                gs = slice(G * GT, (G + 1) * GT)
                for c in range(8):
                    act(sq[c % 2], xT[:, c, gs], AF.Square, [B("xT", c, G)], [B("nsq", c % 2)])
                    mm(PS[0][:, :], cs("ones"), sq[c % 2], c == 0, c == 7, [B("nsq", c % 2), B("cst")], [PB[0]])
                act(lnv, PS[0][:, :], AF.Ln, [PB[0], B("misc")], [B("nln")], bias=misc[:, 0:1], scale=1.0 / D)
                act(rstd, lnv, AF.Exp, [B("nln")], [B("nrstd")], scale=-0.5)
                for c in range(8):
                    stt(tmp[c % 2], xT[:, c, gs], scal[:, o + c:o + c + 1], rstd, ALU.mult, ALU.mult,
                        [B("xT", c, G), B("scal", i), B("nrstd")], [B("ntmp", c % 2)])
                    act(hT[:, c, gs], tmp[c % 2], AF.Identity, [B("ntmp", c % 2), B("scal", i)], [B("hT", c, G)],
                        bias=scal[:, o + 8 + c:o + 9 + c], scale=1.0)
                if G == 0 and i == 0:
                    dump("rstd0", rstd, 512, [B("nrstd")])
                    dump("hT00", hT[:, 0, 0:512], 512, [B("hT", 0, 0)])
                    dump("hT70", hT[:, 7, 0:512], 512, [B("hT", 7, 0)])
            A.release(m0)

        S.add("dve", lambda e: e.memset(misc[:, 0:1], EPS), [], [B("misc")])

        def ffn(i, wi_d, wo_d_):
            o = i * 24
            m0 = A.mark()
            NS = 3
            slab = [(A.alloc([8, 256], BF16, [B("fslab", i_, 0)]), A.alloc([8, 256], BF16, [B("fslab", i_, 1)]),
                     A.alloc([2, 1024], BF16, [B("fslab", i_, 2)])) for i_ in range(NS)]
            sg = [A.alloc([GT], F32, [B("fsg", i_)]) for i_ in range(2)]
            aT = [A.alloc([GT], BF16, [B("faT", i_)]) for i_ in range(4)]
            for s in range(NHC // 2):
                wg, wu, wo_ = slab[s % NS]
                sbg, sbu, sbo = B("fslab", s % NS, 0), B("fslab", s % NS, 1), B("fslab", s % NS, 2)
                j0 = 2 * s
                dma("pool", wg, wi_d[:, j0 * 128:j0 * 128 + 256].rearrange("(k p) n -> p k n", p=128), [], [sbg])
                dma("pool", wu, wi_d[:, DFF + j0 * 128:DFF + j0 * 128 + 256].rearrange("(k p) n -> p k n", p=128),
                    [], [sbu])
                dma("pool", wo_, wo_d_[j0 * 128:j0 * 128 + 256, :].rearrange("(j p) n -> p j n", p=128), [], [sbo])
                for G in range(NG):
                    gs = slice(G * GT, (G + 1) * GT)
                    q = s * NG + G
                    for jj in range(2):
                        it = q * 2 + jj
                        pg, pu = 1 + (it % 2), 3 + (it % 2)
                        a_ = aT[(q % 2) * 2 + jj]
                        ab = B("faT", (q % 2) * 2 + jj)
                        for k in range(8):
                            mm(PS[pg][:, :], wg[:, k, jj * 128:(jj + 1) * 128], hT[:, k, gs], k == 0, k == 7,
                               [sbg, B("hT", k, G)], [PB[pg]])
                        for k in range(8):
                            mm(PS[pu][:, :], wu[:, k, jj * 128:(jj + 1) * 128], hT[:, k, gs], k == 0, k == 7,
                               [sbu, B("hT", k, G)], [PB[pu]])
                        act(sg[it % 2], PS[pg][:, :], AF.Silu, [PB[pg]], [B("fsg", it % 2)])
                        tt("dve", a_, sg[it % 2], PS[pu][:, :], ALU.mult, [B("fsg", it % 2), PB[pu]], [ab])
                    for c2 in range(8):
                        py = 5 + ((q * 8 + c2) % 3)
                        for jj in range(2):
                            a_ = aT[(q % 2) * 2 + jj]
                            ab = B("faT", (q % 2) * 2 + jj)
                            mm(PS[py][:, :], wo_[:, jj, c2 * 128:(c2 + 1) * 128], a_, jj == 0, jj == 1,
                               [sbo, ab], [PB[py]])
                        stt(xT[:, c2, gs], PS[py][:, :], scal[:, o + 16 + c2:o + 17 + c2], xT[:, c2, gs],
                            ALU.mult, ALU.add, [PB[py], B("scal", i), B("xT", c2, G)], [B("xT", c2, G)])
            A.release(m0)

        norm_mod(0)
        ffn(0, w1i_d, w1o_d)
        mod_part([3, 4, 5], (1,))
        if stage >= 2:
            norm_mod(1)
        mod_part([6, 7, 8], (2,))
        A.release(mM0)


        AX = mybir.AxisListType.X

        def mixer():
            mM = A.mark()
            QT = A.alloc([8, NT], BF16, [B("QT", ch, j) for ch in range(8) for j in range(16)])
            Vast = A.alloc([2112], BF16, [B("Vast")])
            Vast4 = Vast.rearrange("p (t g e) -> p t g e", g=2, e=66)
            sm = A.alloc([64], F32, [B("sm")])
            subg = A.alloc([128], F32, [B("subg")])
            ltmp = A.alloc([128], F32, [B("ltmp")])
            QB = lambda ch, G: [B("QT", ch, j) for j in range(4 * G, 4 * G + 4)]
            S.add("dve", lambda e: e.tensor_reduce(out=sm[:, 0:4], in_=cs("qkgb").rearrange("p (a b) -> p a b", b=64),
                                                    axis=AX, op=ALU.max, apply_absolute_value=True),
                  [B("cst")], [B("sm")])
            tt("dve", sm[:, 4:5], sm[:, 0:1], sm[:, 1:2], ALU.mult, [B("sm")], [B("sm")])
            tt("dve", sm[:, 5:6], sm[:, 2:3], sm[:, 3:4], ALU.mult, [B("sm")], [B("sm")])
            ts("dve", sm[:, 4:6], sm[:, 4:6], -8.0, None, ALU.mult, None, [B("sm")], [B("sm")])
            act(sm[:, 8:16], cs("sink"), AF.Exp, [B("cst"), B("sm")], [B("sm")], bias=sm[:, 4:5], scale=1.0)
            tt("dve", ltmp[:, 0:64], cs("lam", 0, 64), cs("lam", 64, 128), ALU.mult, [B("cst")], [B("ltmp")])
            tt("dve", ltmp[:, 64:128], cs("lam", 128, 192), cs("lam", 192, 256), ALU.mult, [B("cst")], [B("ltmp")])
            S.add("dve", lambda e: e.tensor_reduce(out=sm[:, 16:18], in_=ltmp.rearrange("p (a b) -> p a b", b=64),
                                                    axis=AX, op=ALU.add), [B("ltmp")], [B("sm")])
            act(sm[:, 18:20], sm[:, 16:18], AF.Exp, [B("sm")], [B("sm")])
            tt("dve", sm[:, 20:21], sm[:, 18:19], sm[:, 19:20], ALU.subtract, [B("sm")], [B("sm")])
            ts("dve", sm[:, 21:22], sm[:, 20:21], LAM_INIT, -1.0, ALU.add, ALU.mult, [B("sm")], [B("sm")])
            ts("dve", subg, cs("subln"), 1.0 - LAM_INIT, None, ALU.mult, None, [B("cst")], [B("subg")])
            negMa, negMb, sinkexp, neglam = sm[:, 4:5], sm[:, 5:6], sm[:, 8:16], sm[:, 21:22]

            m1 = A.mark()
            sinT = A.alloc([NT], F32, [B("sinT")])
            cosT = A.alloc([NT], F32, [B("cosT")])
            m2 = A.mark()
            posi = A.alloc([NT], I32, [B("posi")])
            ang = A.alloc([NT], F32, [B("ang")])
            ua = A.alloc([NT], F32, [B("ua")])
            ub = A.alloc([NT], F32, [B("ub")])
            dma("sp", posi, pos_d, [], [B("posi")])
            cp("dve", ua, posi, [B("posi")], [B("ua")])
            ts("dve", ang, ua, cs("invf"), None, ALU.mult, None, [B("ua"), B("cst")], [B("ang")])
            for tab, tb, off in ((sinT, B("sinT"), 0.0), (cosT, B("cosT"), 0.25)):
                ts("dve", ua, ang, 1.0 / TWO_PI, off, ALU.mult, ALU.add, [B("ang")], [B("ua")])
                ts("dve", ub, ua, MAGIC, None, ALU.add, None, [B("ua")], [B("ub")])
                ts("dve", ua, ub, -MAGIC, None, ALU.add, None, [B("ub")], [B("ua")])
                stt(ub, ua, -C1, ang, ALU.mult, ALU.add, [B("ua"), B("ang")], [B("ub")])
                stt(ub, ua, -C2, ub, ALU.mult, ALU.add, [B("ua"), B("ub")], [B("ub")])
                ts("dve", ub, ub, off * TWO_PI, PI_CL, ALU.add, ALU.min, [B("ub")], [B("ub")])
                ts("dve", ub, ub, -PI_CL, None, ALU.max, None, [B("ub")], [B("ub")])
                act(tab, ub, AF.Sin, [B("ub")], [tb])
            A.release(m2)
            wq = [A.alloc([8, 128], BF16, [B("wq", i_, 0), B("wq", i_, 1)]) for i_ in range(3)]
            qg = [A.alloc([GT], F32, [B("qg", i_)]) for i_ in range(2)]
            sqq = [A.alloc([GT], F32, [B("sqq", i_)]) for i_ in range(2)]
            rs = [A.alloc([GT], F32, [B("rs", i_)]) for i_ in range(2)]
            t1 = [A.alloc([GT], F32, [B("t1", i_)]) for i_ in range(2)]
            t2 = [A.alloc([GT], F32, [B("t2", i_)]) for i_ in range(2)]
            kst = [A.alloc([GT], BF16, [B("kst", i_)]) for i_ in range(2)]
            chunks = []
            for m_ in range(4):
                chunks.append(("q", OQA + m_ * 128, 0, m_))
            for g in range(2):
                chunks.append(("kd", OKA + g * 64, 1, 4 + g))
            for h in range(4):
                chunks.append(("q", OQB + h * 128, 2, 4 + h))
            for h in range(4):
                chunks.append(("k", OKB + h * 128, 3, h))
            nkc = [0]
            pitems = [(ci, G) for ci in range(len(chunks)) for G in range(NG)]

            def stage_a(n):
                ci, G = pitems[n]
                kind, col, gi, di = chunks[ci]
                w_ = wq[ci % 3]
                wb0, wb1 = B("wq", ci % 3, 0), B("wq", ci % 3, 1)
                if G == 0:
                    if kind == "kd":
                        dma("pool", w_[:, :, 0:64], win_d[:, col:col + 64].rearrange("(k p) n -> p k n", p=128), [], [wb0])
                        dma("pool", w_[:, :, 64:128], win_d[:, col:col + 64].rearrange("(k p) n -> p k n", p=128), [], [wb1])
                    else:
                        dma("pool", w_, win_d[:, col:col + 128].rearrange("(k p) n -> p k n", p=128), [], [wb0, wb1])
                gs = slice(G * GT, (G + 1) * GT)
                i2 = n % 2
                pq = PS[1 + i2]
                for k in range(8):
                    mm(pq, w_[:, k, :], hT[:, k, gs], k == 0, k == 7, [wb0, wb1, B("hT", k, G)], [PB[1 + i2]])
                act(qg[i2], pq, AF.Identity, [PB[1 + i2], B("cst")], [B("qg", i2)], scale=cs("qkg", gi, gi + 1))
                act(sqq[i2], pq, AF.Square, [PB[1 + i2]], [B("sqq", i2)])

            def stage_b(n):
                ci, G = pitems[n]
                kind, col, gi, di = chunks[ci]
                gs = slice(G * GT, (G + 1) * GT)
                i2 = n % 2
                pst, pr = PS[3 + i2], PS[5 + i2]
                mm(pst, cs("bones"), sqq[i2], True, True, [B("sqq", i2), B("cst")], [PB[3 + i2]])
                mm(pr, cs("rotm"), qg[i2], True, True, [B("qg", i2), B("cst")], [PB[5 + i2]])
                act(rs[i2], pst, AF.Ln, [PB[3 + i2], B("misc")], [B("rs", i2)], bias=misc[:, 0:1], scale=1.0 / 64)
                act(rs[i2], rs[i2], AF.Exp, [B("rs", i2)], [B("rs", i2)], scale=-0.5)
                tt("dve", t1[i2], qg[i2], cosT[:, gs], ALU.mult, [B("qg", i2), B("cosT")], [B("t1", i2)])
                tt("dve", t2[i2], pr, sinT[:, gs], ALU.mult, [PB[5 + i2], B("sinT")], [B("t2", i2)])
                tt("pool", t1[i2], t1[i2], t2[i2], ALU.add, [B("t1", i2), B("t2", i2)], [B("t1", i2)])
                if kind == "q":
                    tt("dve", QT[:, di, gs], t1[i2], rs[i2], ALU.mult, [B("t1", i2), B("rs", i2)], QB(di, G))
                else:
                    nk = nkc[0]
                    ks = kst[nk % 2]
                    tt("dve", ks, t1[i2], rs[i2], ALU.mult, [B("t1", i2), B("rs", i2)], [B("kst", nk % 2)])
                    dma("sp", kgi[di][:, gs], ks, [B("kst", nk % 2)], [B("kgi", di, G)])
                    nkc[0] += 1

            stage_a(0)
            for n in range(len(pitems)):
                if n + 1 < len(pitems):
                    stage_a(n + 1)
                stage_b(n)
            dump("qa0", QT[:, 0, 0:512], 512, QB(0, 0))
            dump("qb0", QT[:, 4, 0:512], 512, QB(4, 0))
            dump("qa3l", QT[:, 3, 1536:2048], 512, QB(3, 3))
            dump("sinT", sinT[:, 1536:2048], 512, [B("sinT")])
            dump("cosT", cosT[:, 1536:2048], 512, [B("cosT")])
            A.release(m1)

            if SUB == 1:
                A.release(mM)
                return
            m1 = A.mark()
            wv = A.alloc([8, 640], BF16, [B("wv", 0), B("wv", 1)])
            Vst = A.alloc([4, 2112], BF16, [B("Vst")])
            Vst4 = Vst.rearrange("p h (t e) -> p h t e", e=132)
            dma("pool", wv[:, :, 0:512], win_d[:, OVB:OVB + 512].rearrange("(k p) n -> p k n", p=128), [], [B("wv", 0)])
            dma("pool", wv[:, :, 512:640], win_d[:, OVA:OVA + 128].rearrange("(k p) n -> p k n", p=128), [], [B("wv", 1)])
            S.add("pool", lambda e: e.memset(Vst, 1.0), [], [B("Vst")])
            S.add("pool", lambda e: e.memset(Vast, 1.0), [], [B("Vast")])
            for t_ in range(16):
                G = t_ // 4
                i2 = t_ % 2
                for k in range(8):
                    mm(PS[1 + i2], hT[:, k, t_ * 128:(t_ + 1) * 128], wv[:, k, 0:512], k == 0, k == 7,
                       [B("wv", 0), B("hT", k, G)], [PB[1 + i2]])
                for k in range(8):
                    mm(PS[3 + i2][:, 0:128], hT[:, k, t_ * 128:(t_ + 1) * 128], wv[:, k, 512:640], k == 0, k == 7,
                       [B("wv", 1), B("hT", k, G)], [PB[3 + i2]])
                cp("act", Vst4[:, :, t_, 0:128], PS[1 + i2].rearrange("p (h e) -> p h e", e=128), [PB[1 + i2]], [B("Vst")])
                cp("dve", Vast4[:, t_, :, 0:64], PS[3 + i2][:, 0:128].rearrange("p (g e) -> p g e", e=64),
                   [PB[3 + i2]], [B("Vast")])
            for h in range(4):
                dma("sp", vgi[h], Vst[:, h, :], [B("Vst")], [B("vgi", h)])
            dma("sp", vgi[4], Vast, [B("Vast")], [B("vgi", 4)])
            A.release(m1)

            if SUB == 2:
                A.release(mM)
                return
            groups = [[0, 1, 2, 3], [4, 5, 6, 7]]
            def allgather(i_ap, o_ap, rd, wr):
                if NOCC:
                    for rp_ in range(4):
                        dma("sp", o_ap[rp_ * 128:(rp_ + 1) * 128, :], i_ap, rd, [B("nocc", wr[0].name, rp_)])
                    S.add("sp", lambda e: e.nop(), [B("nocc", wr[0].name, rp_) for rp_ in range(4)], wr)
                    return
                S.add("pool", lambda e: e.collective_compute("AllGather", ALU.bypass, replica_groups=groups,
                                                              ins=[i_ap], outs=[o_ap]), rd, wr, kind="cc")
            for di in (4, 5, 0, 1, 2, 3):
                allgather(kgi[di], kgo[di], [B("kgi", di, G) for G in range(NG)], [B("kgo", di)])
            for h in (4, 0, 1, 2, 3):
                allgather(vgi[h], vgo[h], [B("vgi", h)], [B("vgo", h)])

            m1 = A.mark()
            KAl = A.alloc([2, NT], BF16, [B("KAl")])
            khalo = A.alloc([2, 8, 128], BF16, [B("khalo")])
            vhalo = A.alloc([8, 132], BF16, [B("vhalo")])
            vhalo4 = vhalo.rearrange("p s (g e) -> p s g e", e=66)
            NEW = 4
            ew = [A.alloc([GT], BF16, [B("ew", i_)]) for i_ in range(NEW)]
            den = A.alloc([8], F32, [B("den")])
            oat = [A.alloc([256], BF16, [B("oat", i_)]) for i_ in range(2)]
            for g in range(2):
                dma("sp", KAl[:, g, :], kgi[4 + g], [B("kgi", 4 + g, G) for G in range(NG)], [B("KAl")])
            for rp in range(4):
                for ed in range(2):
                    slot = rp * 2 + ed
                    c0 = 0 if ed == 0 else 1920
                    kc = 0 if ed == 0 else 15
                    for g in range(2):
                        dma("sp", khalo[:, g, slot, :], kgo[4 + g][rp * 128:(rp + 1) * 128, c0:c0 + 128],
                            [B("kgo", 4 + g)], [B("khalo")])
                    dma("sp", vhalo[:, slot, :], vgo[4][rp * 128:(rp + 1) * 128, kc * 132:(kc + 1) * 132],
                        [B("vgo", 4)], [B("vhalo")])
            PSb6 = PSD[3][:, 0:512].bitcast(BF16)
            wun = []
            for j in list(range(1, 15)) + [0, 15]:
                for g in range(2):
                    units = []
                    for jj in (j - 1, j, j + 1):
                        if 0 <= jj <= 15:
                            mk_ = None if jj == j else (("m", mk("mprev")) if jj == j - 1 else ("m", mk("mnext")))
                            units.append((KAl[:, g, jj * 128:(jj + 1) * 128], B("KAl"), Vast4[:, jj, g, 0:65], B("Vast"), mk_))
                    if j == 0 or j == 15:
                        for slot in range(8):
                            so = slot if j == 0 else 8 + slot
                            mk_ = ("s", cs("sel", so, so + 1), mk("mprev") if j == 0 else mk("mnext"))
                            units.append((khalo[:, g, slot, :], B("khalo"), vhalo4[:, slot, g, 0:65], B("vhalo"), mk_))
                    for u, un in enumerate(units):
                        wun.append((j, g, u, len(units)) + un)

            def w_qk(n):
                j, g, u, nu, kap, kb_, vap, vb_, mk_ = wun[n]
                pw = n % 2
                pwb = [PB[2 * pw], PB[2 * pw + 1]]
                for hh in range(4):
                    head = 4 * g + hh
                    ch, half = head // 2, head % 2
                    mm(PS[2 * pw + half][:, (hh // 2) * 128:(hh // 2 + 1) * 128], kap[half * 64:(half + 1) * 64, :],
                       QT[half * 64:(half + 1) * 64, ch, j * 128:(j + 1) * 128], True, True,
                       [kb_, B("QT", ch, j)], [pwb[half]])
                e_ = ew[n % NEW]
                eb = B("ew", n % NEW)
                act(e_.rearrange("p (a b) -> p a b", b=256), PSD[pw].rearrange("p (a b) -> p a b", b=512)[:, :, 0:256],
                    AF.Exp, pwb + [B("sm")], [eb], bias=negMa, scale=0.125)
                if mk_ is not None:
                    if mk_[0] == "m":
                        tt("dve", e_, e_, mk_[1], ALU.mult, [eb, B("msk")], [eb])
                    else:
                        stt(e_, e_, mk_[1], mk_[2], ALU.mult, ALU.mult, [eb, B("cst"), B("msk")], [eb])

            pcc = [0]

            def w_pv(n):
                j, g, u, nu, kap, kb_, vap, vb_, mk_ = wun[n]
                pc = pcc[0]
                pso = PS[4 + pc % 2]
                pob = PB[4 + pc % 2]
                e_ = ew[n % NEW]
                eb = B("ew", n % NEW)
                for hh in range(4):
                    ec_ = (hh % 2) * 256 + (hh // 2) * 128
                    mm(pso[:, hh * 66:hh * 66 + 65], e_[:, ec_:ec_ + 128], vap,
                       (u == 0 and hh == 0), u == nu - 1, [eb, vb_], [pob], skip=True)
                if u != nu - 1:
                    return
                ot = oat[pc % 2]
                otb = B("oat", pc % 2)
                for hh in range(4):
                    ts("dve", den[:, hh:hh + 1], pso[:, hh * 66 + 64:hh * 66 + 65], sinkexp[:, 4 * g + hh:4 * g + hh + 1],
                       None, ALU.add, None, [pob, B("sm")], [B("den")])
                S.add("dve", lambda e, o_=den[:, 4:8], i_=den[:, 0:4]: e.reciprocal(out=o_, in_=i_), [B("den")], [B("den")])
                for hh in range(4):
                    act(ot[:, hh * 64:(hh + 1) * 64], pso[:, hh * 66:hh * 66 + 64], AF.Identity, [pob, B("den")], [otb],
                        scale=den[:, 4 + hh:5 + hh])
                for i_ in range(2):
                    tr(PSb6[:, i_ * 128:(i_ + 1) * 128], ot[:, i_ * 128:(i_ + 1) * 128], identb, [otb, B("msk")], [PB[6]])
                for i_ in range(2):
                    cp("dve", QT[:, 2 * g + i_, j * 128:(j + 1) * 128], PSb6[:, i_ * 128:(i_ + 1) * 128], [PB[6]],
                       [B("QT", 2 * g + i_, j)])
                pcc[0] += 1

            w_qk(0)
            for n in range(len(wun)):
                if n + 1 < len(wun):
                    w_qk(n + 1)
                w_pv(n)
            dump("oa0", QT[:, 0, 0:512], 512, QB(0, 0))
            dump("oa3l", QT[:, 3, 1536:2048], 512, QB(3, 3))
            A.release(m1)

            if SUB == 3:
                A.release(mM)
                return
            m1 = A.mark()
            NKV = 3
            Kp = [A.alloc([NT], BF16, [B("Kp", i_)]) for i_ in range(NKV)]
            Vp = [A.alloc([16, 132], BF16, [B("Vp", i_)]) for i_ in range(NKV)]
            NE = 4
            e2 = [A.alloc([1024], BF16, [B("e2", i_)]) for i_ in range(NE)]
            ESP = 768
            esums = [[A.alloc([1024], F32, [B("esum", p_, a_, 0), B("esum", p_, a_, 1)]) for a_ in range(2)] for p_ in range(2)]
            o1s = A.alloc([GT], F32, [B("o1s")])
            o2s = A.alloc([GT], F32, [B("o2s")])
            hgc = [0]
            pending = []
            frc = [A.alloc([GT], F32, [B("frc", i_)]) for i_ in range(2)]
            fu = A.alloc([GT], F32, [B("fu")])
            fv = A.alloc([GT], F32, [B("fv")])
            sublnT = A.alloc([1], F32, [B("sublnT")])
            tt("dve", fu[:, 0:128], subg, ident, ALU.mult, [B("subg"), B("cst")], [B("fu")])
            S.add("dve", lambda e: e.tensor_reduce(out=sublnT, in_=fu[:, 0:128], axis=AX, op=ALU.add), [B("fu")], [B("sublnT")])
            PSb7 = PSD[3][:, 512:1024].bitcast(BF16)
            pcn = 0
            ec = 0
            fc = 0

            def acc(t, qq):
                idx = t * 4 + qq
                return PS[4 + idx // 3][:, (idx % 3) * 130:(idx % 3) * 130 + 129], PB[4 + idx // 3], idx % 3 == 0

            for h in range(4):
                for G in range(NG):
                    gs = slice(G * GT, (G + 1) * GT)
                    items = [(rp, kc) for rp in range(4) for kc in range(16)]
                    st_ = {}

                    def issue_qk(i, h=h, G=G, gs=gs, st_=st_):
                        nonlocal pcn, ec
                        rp, kc = items[i]
                        if kc == 0:
                            kp, vp = Kp[pcn % NKV], Vp[pcn % NKV]
                            kpb, vpb = B("Kp", pcn % NKV), B("Vp", pcn % NKV)
                            dma("sp", kp, kgo[h][rp * 128:(rp + 1) * 128, :], [B("kgo", h)], [kpb])
                            dma("sp", vp, vgo[h][rp * 128:(rp + 1) * 128, :].rearrange("p (t e) -> p t e", e=132),
                                [B("vgo", h)], [vpb])
                            st_["piece", rp] = (kp, vp, kpb, vpb)
                            pcn += 1
                        kp, vp, kpb, vpb = st_["piece", rp]
                        dpi = ec % 2
                        e_ = e2[ec % NE]
                        eb = B("e2", ec % NE)
                        st_["e", i] = (e_, eb)
                        ec += 1
                        mm(PS[2 * dpi], kp[0:64, kc * 128:(kc + 1) * 128], QT[0:64, 4 + h, gs], True, True,
                           [kpb] + QB(4 + h, G), [PB[2 * dpi]])
                        mm(PS[2 * dpi + 1], kp[64:128, kc * 128:(kc + 1) * 128], QT[64:128, 4 + h, gs], True, True,
                           [kpb] + QB(4 + h, G), [PB[2 * dpi + 1]])
                        act(e_, PSD[dpi][:, :], AF.Exp, [PB[2 * dpi], PB[2 * dpi + 1], B("sm")], [eb], bias=negMb, scale=0.125)

                    def issue_pv(i, st_=st_, h=h, G=G):
                        rp, kc = items[i]
                        kp, vp, kpb, vpb = st_["piece", rp]
                        e_, eb = st_["e", i]
                        last = i == len(items) - 1
                        for t in range(2):
                            mm(PS[4 + t], vp[:, kc, 0:128], e_[:, t * 512:(t + 1) * 512], i == 0, last, [eb, vpb], [PB[4 + t]])
                        par = hgc[0] % 2
                        es = esums[par][i % 2]
                        b0, b1 = B("esum", par, i % 2, 0), B("esum", par, i % 2, 1)
                        if i < 2:
                            cp("dve", es[:, 0:ESP], e_[:, 0:ESP], [eb], [b0])
                            cp("pool", es[:, ESP:1024], e_[:, ESP:1024], [eb], [b1])
                        else:
                            tt("dve", es[:, 0:ESP], es[:, 0:ESP], e_[:, 0:ESP], ALU.add, [eb, b0], [b0])
                            tt("pool", es[:, ESP:1024], es[:, ESP:1024], e_[:, ESP:1024], ALU.add, [eb, b1], [b1])
                        if i >= 2 and pending:
                            pending.pop(0)()

                    issue_qk(0)
                    issue_qk(1)
                    for i in range(len(items)):
                        if i + 2 < len(items):
                            issue_qk(i + 2)
                        issue_pv(i)
                    while pending:
                        pending.pop(0)()
                    cp("act", o1s, PS[4], [PB[4]], [B("o1s")])
                    cp("dve", o2s, PS[5], [PB[5]], [B("o2s")])
                    par = hgc[0] % 2
                    hgc[0] += 1

                    def fin_steps(h=h, G=G, gs=gs, par=par):
                        esb = [B("esum", par, a_, k_) for a_ in range(2) for k_ in range(2)]
                        steps = []
                        for t in range(2):
                            def s_mm(t=t):
                                mm(PS[6 + t], cs("ones"), esums[par][0][:, t * 512:(t + 1) * 512], True, False, esb + [B("cst")], [PB[6 + t]])
                                mm(PS[6 + t], cs("ones"), esums[par][1][:, t * 512:(t + 1) * 512], False, True, esb + [B("cst")], [PB[6 + t]])
                            steps.append(s_mm)
                        for t in range(2):
                            steps.append(lambda t=t: act(frc[t], PS[6 + t], AF.Ln, [PB[6 + t]], [B("frc", t)]))
                            steps.append(lambda t=t: act(frc[t], frc[t], AF.Exp, [B("frc", t)], [B("frc", t)], scale=-1.0))
                        steps.append(lambda: tt("dve", fu, o2s, frc[1], ALU.mult, [B("o2s"), B("frc", 1)], [B("fu")]))
                        steps.append(lambda: tt("dve", fv, o1s, frc[0], ALU.mult, [B("o1s"), B("frc", 0)], [B("fv")]))
                        steps.append(lambda: stt(fv, fu, neglam, fv, ALU.mult, ALU.add, [B("fu"), B("fv"), B("sm")], [B("fv")]))
                        steps.append(lambda: act(fu, fv, AF.Square, [B("fv")], [B("fu")]))
                        steps.append(lambda: mm(PS[6], cs("ones"), fu, True, True, [B("fu"), B("cst")], [PB[6]]))
                        steps.append(lambda: act(frc[0], PS[6], AF.Ln, [PB[6], B("misc")], [B("frc", 0)], bias=misc[:, 0:1], scale=1.0 / 128))
                        steps.append(lambda: act(frc[0], frc[0], AF.Exp, [B("frc", 0)], [B("frc", 0)], scale=-0.5))
                        steps.append(lambda: stt(QT[:, 4 + h, gs], fv, sublnT, frc[0], ALU.mult, ALU.mult,
                                                 [B("fv"), B("frc", 0), B("sublnT")], QB(4 + h, G)))
                        return steps

                    pending.extend(fin_steps())
            while pending:
                pending.pop(0)()
            dump("ob0", QT[:, 4, 0:512], 512, QB(4, 0))
            dump("ob3l", QT[:, 7, 1536:2048], 512, QB(7, 3))
            A.release(m1)

            if SUB == 4:
                A.release(mM)
                return
            m1 = A.mark()
            NW = 2
            wsl = [(A.alloc([8, 128], BF16, [B("wga", i_)]), A.alloc([8, 128], BF16, [B("wgb", i_)]),
                    A.alloc([4, 128], BF16, [B("wba", i_)]), A.alloc([4, 128], BF16, [B("wbb", i_)]),
                    A.alloc([1024], BF16, [B("wo2", i_)])) for i_ in range(NW)]
            sga = [A.alloc([GT], F32, [B("sga", i_)]) for i_ in range(2)]
            sgb = [A.alloc([GT], F32, [B("sgb", i_)]) for i_ in range(2)]
            mT = [A.alloc([GT], BF16, [B("mT", i_)]) for i_ in range(2)]
            zcc = [0]
            mitems = [(c1, G) for c1 in range(8) for G in range(NG)]

            def stage_g(q):
                c1, G = mitems[q]
                wga, wgb, wba, wbb, wo2 = wsl[c1 % NW]
                bs = [B(nm, c1 % NW) for nm in ("wga", "wgb", "wba", "wbb", "wo2")]
                cols = slice(c1 * 128, (c1 + 1) * 128)
                if G == 0:
                    dma("pool", wga, win_d[:, OGA + c1 * 128:OGA + (c1 + 1) * 128].rearrange("(k p) n -> p k n", p=128), [], [bs[0]])
                    dma("pool", wgb, win_d[:, OGB + c1 * 128:OGB + (c1 + 1) * 128].rearrange("(k p) n -> p k n", p=128), [], [bs[1]])
                    dma("pool", wba, wa_d[:, cols].rearrange("(k p) n -> p k n", p=128), [], [bs[2]])
                    dma("pool", wbb, wb_d[:, cols].rearrange("(k p) n -> p k n", p=128), [], [bs[3]])
                    dma("pool", wo2, wo_d[c1 * 128:(c1 + 1) * 128, :], [], [bs[4]])
                gs = slice(G * GT, (G + 1) * GT)
                i2 = q % 2
                for k in range(8):
                    mm(PS[0], wga[:, k, :], hT[:, k, gs], k == 0, k == 7, [bs[0], B("hT", k, G)], [PB[0]])
                for k in range(8):
                    mm(PS[1], wgb[:, k, :], hT[:, k, gs], k == 0, k == 7, [bs[1], B("hT", k, G)], [PB[1]])
                for k in range(4):
                    mm(PS[2], wba[:, k, :], QT[:, k, gs], k == 0, k == 3, [bs[2]] + QB(k, G), [PB[2]])
                for k in range(4):
                    mm(PS[3], wbb[:, k, :], QT[:, 4 + k, gs], k == 0, k == 3, [bs[3]] + QB(4 + k, G), [PB[3]])
                act(sga[i2], PS[0], AF.Sigmoid, [PB[0]], [B("sga", i2)])
                act(sgb[i2], PS[1], AF.Sigmoid, [PB[1]], [B("sgb", i2)])
                tt("dve", sga[i2], sga[i2], PS[2], ALU.mult, [B("sga", i2), PB[2]], [B("sga", i2)])
                tt("dve", sgb[i2], sgb[i2], PS[3], ALU.mult, [B("sgb", i2), PB[3]], [B("sgb", i2)])
                tt("pool", mT[i2], sga[i2], sgb[i2], ALU.add, [B("sga", i2), B("sgb", i2)], [B("mT", i2)])

            def stage_z(q):
                c1, G = mitems[q]
                wo2 = wsl[c1 % NW][4]
                bwo = B("wo2", c1 % NW)
                gs = slice(G * GT, (G + 1) * GT)
                i2 = q % 2
                for c2 in range(8):
                    pz = 4 + zcc[0] % 4
                    mm(PS[pz], wo2[:, c2 * 128:(c2 + 1) * 128], mT[i2], True, True, [bwo, B("mT", i2)], [PB[pz]])
                    stt(xT[:, c2, gs], PS[pz], scal[:, 24 + 16 + c2:24 + 17 + c2], xT[:, c2, gs], ALU.mult, ALU.add,
                        [PB[pz], B("scal", 1), B("xT", c2, G)], [B("xT", c2, G)])
                    zcc[0] += 1

            stage_g(0)
            for q in range(len(mitems)):
                if q + 1 < len(mitems):
                    stage_g(q + 1)
                stage_z(q)
            A.release(m1)
            A.release(mM)

        if stage >= 2:
            mixer()

        if stage >= 3:
            norm_mod(2)
            ffn(2, w2i_d, w2o_d)

        m0 = A.mark()
        ost = [A.alloc([D], F32, [B("ost", i_)]) for i_ in range(2)]
        outs = []
        for t_ in range(16):
            G = t_ // 4
            os_ = ost[t_ % 2]
            for c in range(8):
                pb = 2 * (t_ % 2) + c // 4
                tr(PS[pb][:, (c % 4) * 128:(c % 4 + 1) * 128], xT[:, c, t_ * 128:(t_ + 1) * 128], ident,
                   [B("xT", c, G), B("cst")], [PB[pb]])
            for hf in range(2):
                pb = 2 * (t_ % 2) + hf
                cp("act" if hf else "dve", os_[:, hf * 512:(hf + 1) * 512], PS[pb][:, :], [PB[pb]], [B("ost", t_ % 2)])
            outs.append(dma("sp", out_d[t_ * 128:(t_ + 1) * 128, :], os_, [B("ost", t_ % 2)], [B("out")]))
        A.release(m0)
        S.emit(st, final_ops=outs + dbg_state["ops"])
    nc._dbg_names = dbg_state["names"]
    return nc


def _host_consts(r, inputs):
    b = r // 4
    f32 = np.float32
    cst = np.zeros((128, NCC), f32)

    def put(name, arr):
        lo, hi = CC[name]
        cst[:, lo:hi] = arr

    put("cT", inputs["c"][b].reshape(8, 128).T)
    put("bmod", inputs["b_mod"][0].reshape(72, 128).T)
    put("gains", np.concatenate([inputs[n][0].reshape(8, 128).T for n in ("norm_ffn1", "norm_mix", "norm_ffn2")], 1))
    put("qkg", np.stack([np.tile(inputs[n][0], 2) for n in ("qn_a", "kn_a", "qn_b", "kn_b")], 1))
    put("qkgb", np.broadcast_to(np.concatenate([inputs[n][0] for n in ("qn_a", "kn_a", "qn_b", "kn_b")])[None], (128, 256)))
    put("sink", np.broadcast_to(inputs["sink_a"][0][None], (128, 8)))
    put("lam", np.broadcast_to(np.concatenate([inputs[n][0] for n in ("lam_q1", "lam_k1", "lam_q2", "lam_k2")])[None], (128, 256)))
    put("subln", np.broadcast_to(inputs["subln_b"][0][None], (128, 128)))
    invf = (10000.0 ** (-np.arange(0, 64, 2, dtype=f32) / f32(64))).astype(f32)
    put("invf", np.tile(invf, 4)[:, None])
    put("ident", np.eye(128, dtype=f32))
    put("ones", np.ones((128, 128), f32))
    bo = np.zeros((128, 128), f32)
    bo[:64, :64] = 1
    bo[64:, 64:] = 1
    put("bones", bo)
    rot = np.zeros((128, 128), f32)
    for blk in range(2):
        for d in range(64):
            if d < 32:
                rot[blk * 64 + d + 32, blk * 64 + d] = -1.0
            else:
                rot[blk * 64 + d - 32, blk * 64 + d] = 1.0
    put("rotm", rot)
    msk = np.zeros((128, NMB), f32)
    msk[:, 0:128] = np.eye(128)
    kk = np.arange(128)[:, None]
    qq = np.arange(128)[None, :]
    mprev = (kk >= qq).astype(f32)
    mnext = (kk <= qq).astype(f32)
    msk[:, 128:640] = np.tile(mprev, (1, 4))
    msk[:, 640:1152] = np.tile(mnext, (1, 4))
    rr = r % 4
    sel = np.zeros((128, 16), f32)
    if rr > 0:
        sel[:, (rr - 1) * 2 + 1] = 1.0
    if rr < 3:
        sel[:, 8 + (rr + 1) * 2 + 0] = 1.0
    put("sel", sel)
    return cst, msk.astype(ml_dtypes.bfloat16)


_STAGE = 3


def kernel(**inputs):
    inputs = {k: np.asarray(v) for k, v in inputs.items()}
    nc = build(_STAGE)
    in_maps = []
    for r in range(8):
        b, t0 = r // 4, (r % 4) * NT
        cst, msk = _host_consts(r, inputs)
        in_maps.append({
            "x": np.ascontiguousarray(inputs["x"][b, t0:t0 + NT, :]),
            "pos": np.ascontiguousarray(np.broadcast_to(inputs["positions"][b, t0:t0 + NT][None, :], (128, NT))).astype(np.int32),
            "cst": cst, "msk": msk,
            "w_mod": inputs["w_mod"][0], "w_ffn1_in": inputs["w_ffn1_in"][0], "w_ffn1_out": inputs["w_ffn1_out"][0],
            "w_in": inputs["w_in"][0], "w_branch_a": inputs["w_branch_a"][0], "w_branch_b": inputs["w_branch_b"][0],
            "w_out": inputs["w_out"][0], "w_ffn2_in": inputs["w_ffn2_in"][0], "w_ffn2_out": inputs["w_ffn2_out"][0],
        })
    res = run_bass_kernel_spmd(nc, in_maps, core_ids=list(range(8)))
    out = np.empty((2, 8192, D), np.float32)
    for r in range(8):
        b, t0 = r // 4, (r % 4) * NT
        out[b, t0:t0 + NT, :] = np.asarray(res.results[r]["out"])
    return out
```
